# Optimizing a Trainium2 kernel written in Bass

```python
import math
import jax, jax.numpy as jnp
from jax import lax
import numpy as np

D_MODEL = 2048
BATCH = 4
SEQ = 2048
DEPTH = 4

MEM_LEN = 256
EPS = 1e-6
POOL_DIM = D_MODEL // 2
POOL_WINDOWS = (2, 4, 8, 16)
POOL_GROUPS = len(POOL_WINDOWS)
POOL_GROUP_DIM = POOL_DIM // POOL_GROUPS
HGRN_DIM = D_MODEL // 2
HGRN_EXPAND = 128
HGRN_HEADS = HGRN_DIM // HGRN_EXPAND
HGRN_KEY_DIM = HGRN_EXPAND
HGRN_VAL_DIM = HGRN_DIM // HGRN_HEADS
FORGET_DIM = HGRN_HEADS * HGRN_KEY_DIM
CHUNK = 64
X_HEADS = 4
X_HEAD_DIM = D_MODEL // X_HEADS
D_FF = -(-(8 * D_MODEL) // (3 * 256)) * 256
IN_COLS = POOL_DIM + 2 * FORGET_DIM + 2 * HGRN_DIM + 2 * D_MODEL
IN_SPLITS = (
    POOL_DIM,
    POOL_DIM + FORGET_DIM,
    POOL_DIM + 2 * FORGET_DIM,
    POOL_DIM + 2 * FORGET_DIM + HGRN_DIM,
    POOL_DIM + 2 * FORGET_DIM + 2 * HGRN_DIM,
    POOL_DIM + 2 * FORGET_DIM + 2 * HGRN_DIM + D_MODEL,
)

kernel_name = "hybrid_pool_hgrn2_gated_xattn_swiglu"


def rmsnorm(x, gain):
    x32 = x.astype(jnp.float32)
    y = x32 * lax.rsqrt(jnp.mean(x32 * x32, axis=-1, keepdims=True) + EPS)
    return (y * gain.astype(jnp.float32)).astype(x.dtype)


def causal_multiscale_pool(a, w_group, scale):
    B_, S_, _ = a.shape
    a32 = a.astype(jnp.float32)
    csum = jnp.pad(jnp.cumsum(a32, axis=1), ((0, 0), (1, 0), (0, 0)))
    t = np.arange(S_)
    outs = []
    for g, w in enumerate(POOL_WINDOWS):
        lo, hi = g * POOL_GROUP_DIM, (g + 1) * POOL_GROUP_DIM
        c = csum[..., lo:hi]
        start = np.maximum(t + 1 - w, 0)
        count = jnp.asarray(np.minimum(t + 1, w).astype(np.float32))[None, :, None]
        window_mean = (c[:, 1:] - c[:, start]) / count
        outs.append(window_mean - a32[..., lo:hi])
    pooled = jnp.stack(outs, axis=2).astype(a.dtype)
    mixed = jnp.einsum('bsgc,gcd->bsgd', pooled, w_group)
    return mixed.reshape(B_, S_, POOL_DIM) * scale


def hgrn2_chunked(q, k, v, log_f):
    B_, S_, H, K = q.shape
    V = v.shape[-1]
    nc = S_ // CHUNK

    def to_chunks(z):
        return z.reshape(B_, nc, CHUNK, H, z.shape[-1]).transpose(1, 0, 3, 2, 4)

    qc, kc, vc, gc = to_chunks(q), to_chunks(k), to_chunks(v), to_chunks(log_f)
    causal = jnp.tril(jnp.ones((CHUNK, CHUNK), dtype=bool))

    def step(state, inp):
        q_, k_, v_, g_ = inp
        b = jnp.cumsum(g_, axis=2)
        rel = b[:, :, :, None, :] - b[:, :, None, :, :]
        decay = jnp.exp(jnp.where(causal[:, :, None], rel, -jnp.inf))
        scores = jnp.einsum('bhtsk,bhsk->bhts', decay * q_[:, :, :, None, :], k_)
        o_intra = jnp.einsum('bhts,bhsv->bhtv', scores, v_)
        o_inter = jnp.einsum('bhtk,bhkv->bhtv', q_ * jnp.exp(b), state)
        b_last = b[:, :, -1:, :]
        k_dec = k_ * jnp.exp(b_last - b)
        new_state = state * jnp.exp(b_last[:, :, 0, :])[..., None] \
            + jnp.einsum('bhsk,bhsv->bhkv', k_dec, v_)
        return new_state, o_intra + o_inter

    state0 = jnp.zeros((B_, H, K, V), jnp.float32)
    _, o = lax.scan(step, state0, (qc, kc, vc, gc))
    return o.transpose(1, 0, 3, 2, 4).reshape(B_, S_, H, V)


def hgrn2_branch(z_q, z_f, z_i, z_og, lb, norm_gain):
    B_, S_, _ = z_q.shape
    dt = z_i.dtype
    zf = z_f.astype(jnp.float32)
    lb = lb.astype(jnp.float32)
    q = jax.nn.silu(z_q.astype(jnp.float32))
    log_f = jnp.logaddexp(jnp.log(lb), jnp.log1p(-lb) + jax.nn.log_sigmoid(zf))
    k = (1.0 - lb) * jax.nn.sigmoid(-zf)
    shp_k = (B_, S_, HGRN_HEADS, HGRN_KEY_DIM)
    o = hgrn2_chunked(q.reshape(shp_k), k.reshape(shp_k),
                      z_i.astype(jnp.float32).reshape(B_, S_, HGRN_HEADS, HGRN_VAL_DIM),
                      log_f.reshape(shp_k))
    o = o * lax.rsqrt(jnp.mean(o * o, axis=-1, keepdims=True) + EPS)
    o = o.reshape(B_, S_, HGRN_DIM) * norm_gain.astype(jnp.float32)
    return (o * jax.nn.silu(z_og.astype(jnp.float32))).astype(dt)


def memory_cross_attention(h, mem_n, w_q, w_kv, w_o):
    B_, S_, _ = h.shape
    M_ = mem_n.shape[1]
    q = (h @ w_q).reshape(B_, S_, X_HEADS, X_HEAD_DIM)
    k, v = jnp.split(mem_n @ w_kv, 2, axis=-1)
    k = k.reshape(B_, M_, X_HEADS, X_HEAD_DIM)
    v = v.reshape(B_, M_, X_HEADS, X_HEAD_DIM)
    s = jnp.einsum('bshd,bmhd->bhsm', q, k).astype(jnp.float32) * (X_HEAD_DIM ** -0.5)
    p = jax.nn.softmax(s, axis=-1).astype(h.dtype)
    o = jnp.einsum('bhsm,bmhd->bshd', p, v).reshape(B_, S_, D_MODEL)
    return o @ w_o


def swiglu(h, w_in, w_out):
    g, u = jnp.split(h @ w_in, 2, axis=-1)
    return (jax.nn.silu(g) * u) @ w_out


def setup_inputs(seed: int = 0) -> dict:
    key = jax.random.key(seed)
    ks = jax.random.split(key, 24)
    f32 = jnp.float32

    def nrm(k, shape, fan_in):
        return jax.random.normal(k, shape, f32) * (fan_in ** -0.5)

    def gain(k, shape):
        return 1.0 + 0.02 * jax.random.normal(k, shape, f32)

    return {
        "x": jax.random.normal(ks[0], (BATCH, SEQ, D_MODEL), f32),
        "mem": jax.random.normal(ks[1], (BATCH, MEM_LEN, D_MODEL), f32),
        "w_in": nrm(ks[2], (DEPTH, D_MODEL, IN_COLS), D_MODEL),
        "w_pool_group": nrm(ks[3], (DEPTH, POOL_GROUPS, POOL_GROUP_DIM, POOL_GROUP_DIM), POOL_GROUP_DIM),
        "pool_scale": gain(ks[4], (DEPTH, POOL_DIM)),
        "hgrn_lower_bounds": 0.1 * jax.random.normal(ks[5], (DEPTH, FORGET_DIM), f32),
        "hgrn_norm": gain(ks[6], (DEPTH, HGRN_DIM)),
        "w_branch_pool": nrm(ks[7], (DEPTH, POOL_DIM, D_MODEL), POOL_DIM),
        "w_branch_hgrn": nrm(ks[8], (DEPTH, HGRN_DIM, D_MODEL), HGRN_DIM),
        "w_mix_out": nrm(ks[9], (DEPTH, D_MODEL, D_MODEL), D_MODEL),
        "norm_mix": gain(ks[10], (DEPTH, D_MODEL)),
        "norm_mem": gain(ks[11], (DEPTH, D_MODEL)),
        "norm_cross": gain(ks[12], (DEPTH, D_MODEL)),
        "w_xq": nrm(ks[13], (DEPTH, D_MODEL, D_MODEL), D_MODEL),
        "w_xkv": nrm(ks[14], (DEPTH, D_MODEL, 2 * D_MODEL), D_MODEL),
        "w_xo": nrm(ks[15], (DEPTH, D_MODEL, D_MODEL), D_MODEL),
        "norm_ffn": gain(ks[16], (DEPTH, D_MODEL)),
        "w_ffn_in": nrm(ks[17], (DEPTH, D_MODEL, 2 * D_FF), D_MODEL),
        "w_ffn_out": nrm(ks[18], (DEPTH, D_FF, D_MODEL), D_FF),
        "norm_final": gain(ks[19], (D_MODEL,)),
    }


def reference(x, mem, w_in, w_pool_group, pool_scale, hgrn_lower_bounds, hgrn_norm,
              w_branch_pool, w_branch_hgrn, w_mix_out, norm_mix, norm_mem, norm_cross,
              w_xq, w_xkv, w_xo, norm_ffn, w_ffn_in, w_ffn_out, norm_final):
    lb_all = jnp.cumsum(jax.nn.softmax(hgrn_lower_bounds.astype(jnp.float32), axis=0), axis=0)
    lb_all = lb_all - lb_all[0:1]
    h = x
    for l in range(DEPTH):
        u = rmsnorm(h, norm_mix[l])
        z = u @ w_in[l]
        z_pool, z_q, z_f, z_i, z_og, z_ga, z_gb = jnp.split(z, IN_SPLITS, axis=-1)
        a_out = causal_multiscale_pool(z_pool, w_pool_group[l], pool_scale[l])
        b_out = hgrn2_branch(z_q, z_f, z_i, z_og, lb_all[l], hgrn_norm[l])
        merged = jax.nn.sigmoid(z_ga) * (a_out @ w_branch_pool[l]) \
            + jax.nn.sigmoid(z_gb) * (b_out @ w_branch_hgrn[l])
        h = h + merged @ w_mix_out[l]
        mem_n = rmsnorm(mem, norm_mem[l])
        h = h + memory_cross_attention(rmsnorm(h, norm_cross[l]), mem_n,
                                       w_xq[l], w_xkv[l], w_xo[l])
        h = h + swiglu(rmsnorm(h, norm_ffn[l]), w_ffn_in[l], w_ffn_out[l])
    return rmsnorm(h, norm_final)
```

```python
import os
from contextlib import ExitStack
import numpy as np
import concourse.bass as bass
import concourse.mybir as mybir
from concourse.bass_utils import run_bass_kernel_spmd

F32 = mybir.dt.float32
BF16 = mybir.dt.bfloat16
AF = mybir.ActivationFunctionType
ALU = mybir.AluOpType
RSQ = AF.Abs_reciprocal_sqrt

D = 2048
NT = 1024
NB = 2
TB = 512
MEM = 256
DFF = 5632
EPS = 1e-6
DEPTH = 4
KC = 16
PLC = 88
P_MIX, P_MEM, P_CROSS, P_FFN, P_PSC, P_HN, P_LB = 0, 16, 32, 48, 64, 72, 80
P_FINAL = DEPTH * PLC
P_FLAG = P_FINAL + 16
P_INVC = P_FLAG + 1
NPC = P_INVC + 64
WINDOWS = (2, 4, 8, 16)
WG = 4096
NWB = 3
SELF_SYNC = True
FFB = (12, 12, 12, 8)


class op:
    def __init__(self, name, *a, **k):
        self.name, self.a, self.k = name, a, k

    def __call__(self, e):
        return getattr(e, self.name)(*self.a, **self.k)


class T:
    __slots__ = ("w", "r")

    def __init__(self):
        self.w = {}
        self.r = {}


class Prog:
    def __init__(self, nc, layers, final_norm):
        self.nc = nc
        self.layers = layers
        self.final_norm = final_norm
        self.q = {e: [] for e in ("pe", "act", "dve", "pool", "sp")}
        self.cnt = {}
        self.waited = {e: {} for e in self.q}
        self.sems = {}
        self.pend = {e: ([], []) for e in self.q}
        self.wspecs = []
        self.woff = 0
        self.wi = 0
        self.pa_set = 0
        self.ncc = 0

    def sem(self, key):
        if key not in self.sems:
            self.sems[key] = self.nc.alloc_semaphore(name="s_" + key.replace(":", "_"))
            self.cnt[key] = 0
        return self.sems[key]

    def _deps(self, eng, reads, writes, extra):
        deps = {}
        for t in reads:
            for k, v in t.w.items():
                deps[k] = max(deps.get(k, 0), v)
        for t in writes:
            for k, v in t.w.items():
                deps[k] = max(deps.get(k, 0), v)
            for k, v in t.r.items():
                deps[k] = max(deps.get(k, 0), v)
        for k, v in extra:
            deps[k] = max(deps.get(k, 0), v)
        for k, v in deps.items():
            if k == eng and (eng == "pe" or not SELF_SYNC):
                continue
            if self.waited[eng].get(k, 0) >= v:
                continue
            self.waited[eng][k] = v
            s = self.sem(k)
            self.q[eng].append(op("wait_ge", s, v))

    def emit(self, eng, fn, reads=(), writes=(), extra=(), signal=True):
        self._deps(eng, reads, writes, extra)
        pr, pw = self.pend[eng]
        pr.extend(reads)
        pw.extend(writes)
        if not signal:
            self.q[eng].append(lambda e, fn=fn: fn(e))
            return None
        s = self.sem(eng)
        self.cnt[eng] += 1
        n = self.cnt[eng]
        self.q[eng].append(lambda e, fn=fn, s=s: fn(e).then_inc(s, 1))
        for t in pr:
            t.r[eng] = n
        for t in pw:
            t.w[eng] = n
        self.pend[eng] = ([], [])
        return (eng, n)

    def dma(self, eng, key, fn, reads=(), writes=(), extra=()):
        self._deps(eng, reads, writes, extra)
        s = self.sem(key)
        self.cnt[key] += 16
        n = self.cnt[key]
        self.q[eng].append(lambda e, fn=fn, s=s: fn(e).then_inc(s, 16))
        for t in reads:
            t.r[key] = n
        for t in writes:
            t.w[key] = n
        return (key, n)


class StopBuild(Exception):
    pass


def build(layers, final_norm, ncores=8, dbg=None):
    nc = bass.Bass("TRN2", target_bir_lowering=False)
    P = Prog(nc, layers, final_norm)
    P.ncores = ncores
    nl = len(layers)
    es = ExitStack()

    def dram(name, shape, kind, dt=F32):
        return nc.dram_tensor(name, shape, dt, kind=kind).ap()

    def sb(name, shape, dt):
        return es.enter_context(nc.sbuf_tensor(name, shape, dt))

    def ps(name, shape, dt):
        return es.enter_context(nc.psum_tensor(name, shape, dt))

    d_h = dram("hin", [128, KC, NT], "ExternalInput")
    d_mem = dram("memt", [128, KC, MEM], "ExternalInput")
    d_par = dram("par", [128, NPC], "ExternalInput")
    d_cst = dram("cst", [128, 128 + 512 + 1024], "ExternalInput")
    d_out = dram("hout", [128, KC, NT], "ExternalOutput")
    d_w = None
    cc_inA = [dram(f"ccinA{i}", [128, 640], "Internal") for i in range(nl)]
    cc_outA = [dram(f"ccoutA{i}", [256, 640], "Internal") for i in range(nl)]
    cc_inB = [dram(f"ccinB{i}", [128, 512], "Internal") for i in range(nl)]
    cc_outB = [dram(f"ccoutB{i}", [256, 512], "Internal") for i in range(nl)]

    H = sb("H", [128, KC, NT], F32)
    U = sb("U", [128, KC, NT], BF16)
    AB = sb("AB", [128, 16, NT], BF16)
    X = sb("X", [128, 16, NT], BF16)
    WB = [sb(f"W{i}", [128, WG], BF16) for i in range(NWB)]
    RS = sb("RS", [128, NT], F32)
    PAR = sb("PAR", [128, NPC], F32)
    MEMH = sb("MEMH", [128, KC, MEM], BF16)
    IDENT = sb("IDENT", [128, 128], BF16)
    ONES = sb("ONES", [128, 128], BF16)
    TRI = sb("TRI", [128, 512], BF16)
    SMASK = sb("SMASK", [128, NT], BF16)
    PFX = sb("PFX", [128, 16], F32)
    SF = sb("SF", [128, 8 * 128], F32)
    ZH = sb("ZH", [128, 128], F32)
    TST = [sb(f"TST{i}", [128, 128], F32) for i in range(2)]
    DC = sb("DC", [128, 16], F32)
    GT = sb("GT", [128, 17], F32)
    ZER = sb("ZER", [128, 16], F32)
    LB = sb("LB", [128, 4, 8], F32)
    OML = sb("OML", [128, 4, 8], F32)
    LBE = sb("LBE", [128, 4, 8], F32)
    LBT = sb("LBT", [128, 3, 8], F32)
    PA = ps("PA", [128, 4, 512], F32)
    PM = ps("PM", [128, 4, 512], F32)

    Ht = [T() for _ in range(KC)]
    Ut = [T() for _ in range(KC)]
    ABt = [T() for _ in range(16)]
    Xt = [T() for _ in range(16)]
    WBt = [T() for _ in range(NWB)]
    PAt = [T() for _ in range(4)]
    PMt = [T() for _ in range(4)]
    RSt, PARt, MEMHt, CSTt, ZHt, DCt, LBt_ = T(), T(), T(), T(), T(), T(), T()
    SFh = [T(), T()]
    TSTt = [T(), T()]
    GTt = T()
    PFXt = T()

    def Xb(j, n=1):
        return X[:, j:j + n, :].rearrange("p c t -> p (c t)")

    def Xf(j, n=2):
        return X[:, j:j + n, :].rearrange("p c t -> p (c t)").bitcast(F32)

    wsrc = []

    def load_w(kc, ncols, srcs):
        g = kc * ncols
        assert g <= WG and sum(s[3] for s in srcs) == ncols
        i = P.wi % NWB
        P.wi += 1
        off = P.woff
        P.woff += g
        buf = WB[i]
        li = P.cur_l
        P.wspecs.append((li, off, kc, ncols, srcs))

        def fn(e, buf=buf, off=off, g=g, li=li):
            return e.dma_start(out=buf[:, 0:g], in_=wsrc[0][li, :, off:off + g], max_dma_last_dim=4096)
        P.dma("pool", f"d:w{i}", fn, writes=[WBt[i]])
        return buf[:, 0:g].rearrange("p (k c) -> p k c", c=ncols), WBt[i]

    def act(fn, reads=(), writes=()):
        return P.emit("act", fn, reads, writes)

    def dve(fn, reads=(), writes=()):
        return P.emit("dve", fn, reads, writes)

    def mm(out, lhsT, rhs, start, stop, reads=(), writes=(), signal=False):
        return P.emit("pe", op("matmul", out, lhsT, rhs, start=start, stop=stop), reads, writes, signal=signal)

    def proj(w3, wt, kc, mcols, rhs_fn, rhs_tiles_fn, evac, n=TB, ntb=NB):
        for mi, c0 in enumerate(mcols):
            st = P.pa_set
            P.pa_set ^= 1
            for k in range(kc):
                for tb in range(ntb):
                    bank = st * 2 + tb
                    mm(PA[:, bank, 0:n], w3[:, k, c0:c0 + 128], rhs_fn(k, tb), k == 0, k == kc - 1,
                       reads=[wt] + rhs_tiles_fn(k), writes=[PAt[bank]], signal=(k == kc - 1))
            for tb in range(ntb):
                bank = st * 2 + tb
                evac(mi, tb, PA[:, bank, 0:n], PAt[bank])

    def norm_stats():
        for k in range(KC):
            j = k % 2
            act(op("activation", out=Xb(j), in_=H[:, k, :], func=AF.Square), [Ht[k]], [Xt[j]])
            for tb in range(NB):
                mm(PM[:, tb, :], ONES[:, :], Xb(j)[:, tb * TB:(tb + 1) * TB], k == 0, k == KC - 1,
                   reads=[Xt[j], CSTt], writes=[PMt[tb]], signal=True)
        for tb in range(NB):
            act(op("activation", out=PM[:, tb, :], in_=PM[:, tb, :], func=AF.Ln, scale=1.0 / D, bias=EPSC[:, 0:1]),
                [PMt[tb], CSTt], [PMt[tb]])
            act(op("activation", out=PM[:, tb, :], in_=PM[:, tb, :], func=AF.Exp, scale=-0.5), [PMt[tb]], [PMt[tb]])

    def norm_apply(gcol, dst=None):
        for k in range(KC):
            for tb in range(NB):
                sl = slice(tb * TB, (tb + 1) * TB)
                if dst is None:
                    dve(op("scalar_tensor_tensor", out=U[:, k, sl], in0=H[:, k, sl], scalar=PAR[:, gcol + k:gcol + k + 1],
                           in1=PM[:, tb, :], op0=ALU.mult, op1=ALU.mult), [Ht[k], PMt[tb], PARt], [Ut[k]])
                else:
                    dve(op("scalar_tensor_tensor", out=H[:, k, sl], in0=H[:, k, sl], scalar=PAR[:, gcol + k:gcol + k + 1],
                           in1=PM[:, tb, :], op0=ALU.mult, op1=ALU.mult), [Ht[k], PMt[tb], PARt], [Ht[k]])

    def rmsnorm_U(gcol):
        norm_stats()
        norm_apply(gcol)

    def evac_resid(m0):
        def f(mi, tb, pap, pt):
            m = m0 + mi
            dve(op("tensor_tensor", out=H[:, m, tb * TB:(tb + 1) * TB], in0=pap, in1=H[:, m, tb * TB:(tb + 1) * TB],
                                          op=ALU.add), [pt, Ht[m]], [Ht[m]])
        return f

    EPSC = sb("EPSC", [128, 2], F32)

    def dump(stage, items):
        if dbg != stage:
            return
        for k, (src, tl) in enumerate(items):
            n = src.shape[-1]
            act(op("activation", out=H[:, k, 0:n], in_=src, func=AF.Copy), list(tl), [Ht[k]])
        raise StopBuild()

    P.cur_l = 0
    P.dma("sp", "d:in", op("dma_start", out=PAR[:, :], in_=d_par[:, :]), writes=[PARt])
    for k in range(KC):
        P.dma("sp", "d:in", op("dma_start", out=H[:, k, :], in_=d_h[:, k, :]), writes=[Ht[k]])
    CF = Xf(0, 4)
    P.dma("sp", "d:in", op("dma_start", out=CF[:, 0:1664], in_=d_cst[:, :]), writes=Xt[0:4])
    MF = Xf(4, 8).rearrange("p (k m) -> p k m", m=MEM)
    P.dma("sp", "d:in", op("dma_start", out=MF, in_=d_mem[:, :, :]), writes=Xt[4:12])
    for t in [PARt] + Ht + Xt[0:12]:
        t.w["d:in"] = P.cnt["d:in"]
    dve(op("tensor_copy", out=IDENT[:, :], in_=CF[:, 0:128]), Xt[0:4], [CSTt])
    dve(op("tensor_copy", out=TRI[:, :], in_=CF[:, 128:640]), Xt[0:4], [CSTt])
    dve(op("tensor_copy", out=SMASK[:, :], in_=CF[:, 640:1664]), Xt[0:4], [CSTt])
    dve(op("memset", ONES[:, :], 1.0), [], [CSTt])
    dve(op("memset", EPSC[:, :], EPS), [], [CSTt])
    dve(op("memset", ZER[:, :], 0.0), [], [CSTt])
    dve(op("memset", GT[:, :], 1.0), [], [GTt])
    for l in range(4):
        act(op("activation", out=LBE[:, l, :], in_=PAR[:, l * PLC + P_LB:l * PLC + P_LB + 8], func=AF.Exp),
            [PARt], [LBt_])
    dve(op("tensor_tensor", out=LBT[:, 0, :], in0=LBE[:, 1, :], in1=LBE[:, 2, :], op=ALU.add), [LBt_], [LBt_])
    dve(op("tensor_tensor", out=LBT[:, 1, :], in0=LBT[:, 0, :], in1=LBE[:, 3, :], op=ALU.add), [LBt_], [LBt_])
    dve(op("tensor_tensor", out=LBT[:, 2, :], in0=LBT[:, 1, :], in1=LBE[:, 0, :], op=ALU.add), [LBt_], [LBt_])
    dve(op("reciprocal", out=LBT[:, 2, :], in_=LBT[:, 2, :]), [LBt_], [LBt_])
    dve(op("memset", LB[:, 0, :], 0.0), [], [LBt_])
    dve(op("tensor_tensor", out=LB[:, 1, :], in0=LBE[:, 1, :], in1=LBT[:, 2, :], op=ALU.mult), [LBt_], [LBt_])
    dve(op("tensor_tensor", out=LB[:, 2, :], in0=LBT[:, 0, :], in1=LBT[:, 2, :], op=ALU.mult), [LBt_], [LBt_])
    dve(op("tensor_tensor", out=LB[:, 3, :], in0=LBT[:, 1, :], in1=LBT[:, 2, :], op=ALU.mult), [LBt_], [LBt_])
    dve(op("tensor_scalar", out=OML[:, :, :], in0=LB[:, :, :], scalar1=-1.0, scalar2=1.0, op0=ALU.mult, op1=ALU.add),
        [LBt_], [LBt_])
    for k in range(KC):
        j = 12 + k % 2
        act(op("activation", out=Xb(j)[:, 0:MEM], in_=MF[:, k, :], func=AF.Square), Xt[4:12], [Xt[j]])
        mm(PM[:, 0, 0:MEM], ONES[:, :], Xb(j)[:, 0:MEM], k == 0, k == KC - 1, reads=[Xt[j], CSTt], writes=[PMt[0]], signal=True)
    act(op("activation", out=RS[:, 0:MEM], in_=PM[:, 0, 0:MEM], func=AF.Sqrt, scale=1.0 / D, bias=EPSC[:, 0:1]),
        [PMt[0], CSTt], [RSt])
    dve(op("reciprocal", out=RS[:, 0:MEM], in_=RS[:, 0:MEM]), [RSt], [RSt])
    for k in range(KC):
        dve(op("tensor_tensor", out=MEMH[:, k, :], in0=MF[:, k, :], in1=RS[:, 0:MEM], op=ALU.mult),
            Xt[4:12] + [RSt], [MEMHt])

    try:
      for li, l in enumerate(layers):
          P.cur_l = li
          P.woff = 0
          pb = l * PLC
          rmsnorm_U(pb + P_MIX)
          dump("U", [(U[:, k, :], [Ut[k]]) for k in range(KC)])
          FB, LG = Xf(0), Xf(2)
          tFB, tLG = Xt[0:2], Xt[2:4]
          VT, KTT, SC, SBF = Xb(10), Xb(11), Xb(12), Xb(13, 2)
          tSBF = Xt[13:15]

          def hset(h):
            s_ = h % 2
            return (Xb(4 + 3 * s_), Xb(5 + 3 * s_), Xb(6 + 3 * s_), Xt[4 + 3 * s_], Xt[5 + 3 * s_], Xt[6 + 3 * s_])

          def hg_proj(h):
            QT, KT, VF, tQT, tKT, tVF = hset(h)
            w3, wt = load_w(KC, 256, [("w_in", 0, 1024 + h * 128, 128), ("w_in", 0, 2048 + h * 128, 128)])

            def ev_qf(mi, tb, pap, pt):
                sl = slice(tb * TB, (tb + 1) * TB)
                if mi == 0:
                    act(op("activation", out=QT[:, sl], in_=pap, func=AF.Silu), [pt], [tQT])
                else:
                    act(op("activation", out=FB[:, sl], in_=pap, func=AF.Sigmoid), [pt], tFB)
            proj(w3, wt, KC, [0, 128], lambda k, tb: U[:, k, tb * TB:(tb + 1) * TB], lambda k: [Ut[k]], ev_qf)
            w3, wt = load_w(KC, 128, [("w_in", 0, 3072 + h * 128, 128)])

            def ev_v(mi, tb, pap, pt):
                sl = slice(tb * TB, (tb + 1) * TB)
                act(op("activation", out=VF[:, sl], in_=pap, func=AF.Copy), [pt], [tVF])
            proj(w3, wt, KC, [0], lambda k, tb: U[:, k, tb * TB:(tb + 1) * TB], lambda k: [Ut[k]], ev_v)

          def hg_stage1(h):
            QT, KT, VF, tQT, tKT, tVF = hset(h)
            lbc = LB[:, l, h:h + 1]
            omlc = OML[:, l, h:h + 1]
            dve(op("tensor_scalar", out=FB, in0=FB, scalar1=omlc, scalar2=lbc, op0=ALU.mult, op1=ALU.add), tFB + [LBt_], tFB)
            act(op("activation", out=LG, in_=FB, func=AF.Ln), tFB, tLG)
            dve(op("tensor_scalar", out=KT, in0=FB, scalar1=-1.0, scalar2=1.0, op0=ALU.mult, op1=ALU.add), tFB, [tKT])
            dve(op("tensor_tensor_scan", out=FB, data0=SMASK[:, :], data1=LG, initial=0.0, op0=ALU.mult, op1=ALU.add),
                tLG + [CSTt], tFB)
            act(op("activation", out=DC[:, :], in_=FB.rearrange("p (c t) -> p c t", t=64)[:, :, 63], func=AF.Exp), tFB, [DCt])
            act(op("activation", out=LG, in_=FB, func=AF.Exp), tFB, tLG)
            act(op("activation", out=FB, in_=FB, func=AF.Exp, scale=-1.0), tFB, tFB)
            dve(op("tensor_tensor_scan", out=GT[:, 1:17], data0=DC[:, :], data1=ZER[:, :], initial=1.0, op0=ALU.mult, op1=ALU.add),
                [DCt, CSTt], [GTt])
            dve(op("tensor_tensor", out=QT, in0=QT, in1=LG, op=ALU.mult), [tQT] + tLG, [tQT])
            dve(op("tensor_tensor", out=KT, in0=KT, in1=FB, op=ALU.mult), [tKT] + tFB, [tKT])
            dve(op("tensor_tensor", out=AB[:, h, :].rearrange("p (c t) -> p c t", t=64), in0=QT.rearrange("p (c t) -> p c t", t=64),
                   in1=GT[:, 0:16].unsqueeze(2).broadcast_to([128, 16, 64]), op=ALU.mult), [tQT, GTt], [ABt[h]])

          def hg_stage2(h):
            QT, KT, VF, tQT, tKT, tVF = hset(h)
            PMb1 = PM[:, 1, :].bitcast(BF16)
            PMb2 = PM[:, 2, :].bitcast(BF16)
            for j in range(8):
                P.emit("pe", op("transpose", PMb1[:, j * 128:(j + 1) * 128], VF[:, j * 128:(j + 1) * 128], IDENT[:, :]),
                       [tVF, CSTt], [PMt[1]], signal=(j == 7))
            for j in range(8):
                P.emit("pe", op("transpose", PMb2[:, j * 128:(j + 1) * 128], KT[:, j * 128:(j + 1) * 128], IDENT[:, :]),
                       [tKT, CSTt], [PMt[2]], signal=(j == 7))
            act(op("activation", out=VT, in_=PMb1, func=AF.Copy), [PMt[1]], [Xt[10]])
            dve(op("tensor_copy", out=KTT, in_=PMb2), [PMt[2]], [Xt[11]])
            for c in range(16):
                j, par = c // 2, c % 2
                ps_ = slice(par * 64, par * 64 + 64)
                mm(PM[ps_, 0, j * 64:(j + 1) * 64], KT[:, c * 64:(c + 1) * 64], QT[:, c * 64:(c + 1) * 64], True, True,
                   reads=[tQT, tKT], writes=[PMt[0]], signal=(c == 15))
            dve(op("tensor_tensor", out=SC[:, 0:512], in0=PM[:, 0, :], in1=TRI[:, :], op=ALU.mult), [PMt[0], CSTt], [Xt[12]])
            def o_chunk(c):
                j, par = c // 2, c % 2
                ps_ = slice(par * 64, par * 64 + 64)
                tb, cc = c // 8, c % 8
                mm(PM[:, 1 + tb, cc * 64:(cc + 1) * 64], VT[ps_, j * 128:(j + 1) * 128], SC[ps_, j * 64:(j + 1) * 64], True, c == 0,
                   reads=[Xt[10], Xt[12]], writes=[PMt[1 + tb]], signal=(c == 0))
                if c > 0:
                    mm(PM[:, 1 + tb, cc * 64:(cc + 1) * 64], SBF[:, c * 128:(c + 1) * 128], QT[:, c * 64:(c + 1) * 64], False, True,
                       reads=tSBF + [tQT], writes=[PMt[1 + tb]], signal=True)
            LAG = 3
            for c in range(16):
                j, par = c // 2, c % 2
                ps_ = slice(par * 64, par * 64 + 64)
                slot = c % 4
                mm(PM[:, 3, slot * 128:(slot + 1) * 128], KTT[ps_, j * 128:(j + 1) * 128], VT[ps_, j * 128:(j + 1) * 128], True, True,
                   reads=[Xt[10], Xt[11]], writes=[PMt[3]], signal=True)
                cur, prv = TST[c % 2], TST[(c + 1) % 2]
                if c == 0:
                    dve(op("tensor_copy", out=cur[:, :], in_=PM[:, 3, slot * 128:(slot + 1) * 128]), [PMt[3]], [TSTt[c % 2]])
                else:
                    dve(op("scalar_tensor_tensor", out=cur[:, :], in0=prv[:, :], scalar=DC[:, c - 1:c],
                           in1=PM[:, 3, slot * 128:(slot + 1) * 128], op0=ALU.mult, op1=ALU.add),
                        [PMt[3], TSTt[(c + 1) % 2], DCt], [TSTt[c % 2]])
                if c < 15:
                    act(op("activation", out=SBF[:, (c + 1) * 128:(c + 2) * 128], in_=cur[:, :], func=AF.Copy, scale=DC[:, c:c + 1]),
                        [TSTt[c % 2], DCt], tSBF)
                else:
                    act(op("activation", out=SF[:, h * 128:(h + 1) * 128], in_=cur[:, :], func=AF.Copy, scale=DC[:, c:c + 1]),
                        [TSTt[c % 2], DCt], [SFh[h // 4]])
                if c >= LAG:
                    o_chunk(c - LAG)
            for c in range(16 - LAG, 16):
                o_chunk(c)
            for tb in range(NB):
                act(op("activation", out=AB[:, 8 + h, tb * TB:(tb + 1) * TB], in_=PM[:, 1 + tb, :], func=AF.Copy),
                    [PMt[1 + tb]], [ABt[8 + h]])

          groups = [[2 * i, 2 * i + 1] for i in range(P.ncores // 2)]
          scc = P.sem("cc")

          def exchange(key, cin, cout, pieces):
              tk = None
              for (apx, tl, c0, ncl) in pieces:
                  tk = P.dma("sp", key, op("dma_start", out=cin[:, c0:c0 + ncl], in_=apx), reads=[tl])
              P._deps("pool", [], [], [tk])
              P.ncc += 1
              ncc = P.ncc
              ccop = op("collective_compute", "AllGather", ALU.bypass, replica_groups=groups, ins=[cin], outs=[cout])
              P.q["pool"].append(lambda e, ccop=ccop, scc=scc: ccop(e).then_inc(scc))
              P.cnt["cc"] = ncc
              for (apx, tl, c0, ncl) in pieces:
                  P.dma("sp", key, op("dma_start", out=apx, in_=cout[0:128, c0:c0 + ncl]), writes=[tl], extra=[("cc", ncc)])
              for (apx, tl, c0, ncl) in pieces:
                  tl.w[key] = P.cnt[key]

          def halo(c):
              w3, wt = load_w(KC, 128, [("w_in", 0, c * 128, 128)])

              def ev_h(mi, tb, pap, pt, c=c):
                  act(op("activation", out=ZH[:, c * 16:(c + 1) * 16], in_=pap, func=AF.Copy), [pt], [ZHt])
              proj(w3, wt, KC, [0], lambda k, tb: U[:, k, NT - 16:NT], lambda k: [Ut[k]], ev_h, n=16, ntb=1)

          hg_proj(0)
          for h in range(8):
              hg_stage1(h)
              if h < 7:
                  hg_proj(h + 1)
              if h < 4:
                  halo(2 * h)
                  halo(2 * h + 1)
              hg_stage2(h)
              if h == 3:
                  exchange("d:exA", cc_inA[li], cc_outA[li], [(SF[:, 0:512], SFh[0], 0, 512), (ZH[:, :], ZHt, 512, 128)])
          exchange("d:exB", cc_inB[li], cc_outB[li], [(SF[:, 512:1024], SFh[1], 0, 512)])
          dump("EX", [(SF[:, :], SFh)] + [(AB[:, 8 + k, :], [ABt[8 + k]]) for k in range(8)] + [(AB[:, k, :], [ABt[k]]) for k in range(7)])
          flag = PAR[:, P_FLAG:P_FLAG + 1]
          SIN = Xb(0)
          dve(op("tensor_scalar", out=SIN[:, 0:512], in0=SF[:, 0:512], scalar1=flag, scalar2=None, op0=ALU.mult), [SFh[0], PARt], [Xt[0]])
          dve(op("tensor_scalar", out=ZH[:, :], in0=ZH[:, :], scalar1=flag, scalar2=None, op0=ALU.mult), [ZHt, PARt], [ZHt])
          XFL = X[:, :, :].rearrange("p c t -> p (c t)")
          ZP = XFL[:, 1024:3104].bitcast(F32)
          T1 = XFL[:, 3104:5184].bitcast(F32)
          T2 = XFL[:, 5184:7264].bitcast(F32)
          tZP, tT1, tT2 = Xt[1:4], Xt[3:6], Xt[5:8]
          OGB = X[:, 8:10, :]
          OGt = [Xt[8], Xt[9]]
          OT, OSQ = Xf(12), Xb(14)
          tOT = Xt[12:14]
          n = NT + 16

          def pe_proj(c):
              w3, wt = load_w(KC, 256, [("w_in", 0, c * 128, 128), ("w_in", 0, 4096 + c * 128, 128)])

              def ev_po(mi, tb, pap, pt, c=c):
                  if mi == 0:
                      act(op("activation", out=ZP[:, 16 + tb * TB:16 + (tb + 1) * TB], in_=pap, func=AF.Copy), [pt], tZP)
                  else:
                      act(op("activation", out=OGB[:, c % 2, tb * TB:(tb + 1) * TB], in_=pap, func=AF.Silu), [pt], [OGt[c % 2]])
              proj(w3, wt, KC, [0, 128], lambda k, tb: U[:, k, tb * TB:(tb + 1) * TB], lambda k: [Ut[k]], ev_po)

          def corr_a(c):
              for tb in range(NB):
                  mm(PM[:, tb, :], SIN[:, c * 128:(c + 1) * 128], AB[:, c, tb * TB:(tb + 1) * TB], True, True,
                     reads=[Xt[0], ABt[c]], writes=[PMt[tb]], signal=True)
                  dve(op("tensor_tensor", out=OT[:, tb * TB:(tb + 1) * TB], in0=PM[:, tb, :], in1=AB[:, 8 + c, tb * TB:(tb + 1) * TB],
                         op=ALU.add), [PMt[tb], ABt[8 + c]], tOT)
              act(op("activation", out=OSQ, in_=OT, func=AF.Square), tOT, [Xt[14]])

          def corr_b(c):
              for tb in range(NB):
                  mm(PM[:, 2 + tb, :], ONES[:, :], OSQ[:, tb * TB:(tb + 1) * TB], True, True, reads=[Xt[14], CSTt], writes=[PMt[2 + tb]],
                     signal=True)
                  act(op("activation", out=PM[:, 2 + tb, :], in_=PM[:, 2 + tb, :], func=AF.Ln, scale=1.0 / 128, bias=EPSC[:, 0:1]),
                      [PMt[2 + tb], CSTt], [PMt[2 + tb]])
                  act(op("activation", out=PM[:, 2 + tb, :], in_=PM[:, 2 + tb, :], func=AF.Exp, scale=-0.5), [PMt[2 + tb]], [PMt[2 + tb]])
              for tb in range(NB):
                  sl = slice(tb * TB, (tb + 1) * TB)
                  dve(op("scalar_tensor_tensor", out=OT[:, sl], in0=OT[:, sl], scalar=PAR[:, pb + P_HN + c:pb + P_HN + c + 1],
                         in1=PM[:, 2 + tb, :], op0=ALU.mult, op1=ALU.mult), tOT + [PMt[2 + tb], PARt], tOT)
              dve(op("tensor_tensor", out=AB[:, 8 + c, :], in0=OT, in1=OGB[:, c % 2, :], op=ALU.mult), tOT + [OGt[c % 2]], [ABt[8 + c]])

          def pool_dve(c):
              g = c // 2
              wlen = WINDOWS[g]
              dve(op("tensor_copy", out=ZP[:, 0:16], in_=ZH[:, c * 16:(c + 1) * 16]), [ZHt], tZP)
              src, tsrc = ZP, tZP
              dsts = [(T1, tT1), (T2, tT2)]
              sh = 1
              di = 0
              while sh < wlen:
                  dst, tdst = dsts[di]
                  di ^= 1
                  lo = 2 * sh - 1
                  dve(op("tensor_tensor", out=dst[:, lo:n], in0=src[:, lo:n], in1=src[:, lo - sh:n - sh], op=ALU.add), tsrc, tdst)
                  src, tsrc = dst, tdst
                  sh *= 2
              PLb = Xb(10 + (c % 2))
              dve(op("scalar_tensor_tensor", out=PLb[:, :], in0=src[:, 16:n], scalar=1.0 / wlen, in1=ZP[:, 16:n], op0=ALU.mult,
                     op1=ALU.subtract), tsrc + tZP, [Xt[10 + (c % 2)]])
              dve(op("tensor_tensor", out=PFX[:, :], in0=src[:, 16:32], in1=PAR[:, P_INVC + g * 16:P_INVC + g * 16 + 16], op=ALU.mult),
                  tsrc + [PARt, PFXt], [PFXt])
              dve(op("tensor_tensor", out=PLb[:, 0:16], in0=PFX[:, :], in1=ZP[:, 16:32], op=ALU.subtract), [PFXt] + tZP, [Xt[10 + (c % 2)]])

          def pool_mix(c):
              g = c // 2
              w3g, wtg = load_w(2, 256, [("w_pool", g * 256, 0, 256)])

              def ev_a(mi, tb, pap, pt, g=g):
                  ch = 2 * g + mi
                  act(op("activation", out=AB[:, ch, tb * TB:(tb + 1) * TB], in_=pap, func=AF.Copy,
                         scale=PAR[:, pb + P_PSC + ch:pb + P_PSC + ch + 1]), [pt, PARt], [ABt[ch]])
              proj(w3g, wtg, 2, [0, 128], lambda k, tb: Xb(10 + k)[:, tb * TB:(tb + 1) * TB], lambda k: [Xt[10 + k]], ev_a)

          pe_proj(0)
          for c in range(8):
              if c == 4:
                  dve(op("tensor_scalar", out=SIN[:, 512:1024], in0=SF[:, 512:1024], scalar1=flag, scalar2=None, op0=ALU.mult),
                      [SFh[1], PARt], [Xt[0]])
              corr_a(c)
              pool_dve(c)
              if c < 7:
                  pe_proj(c + 1)
              corr_b(c)
              if c % 2 == 1:
                  pool_mix(c)
          dump("AB", [(AB[:, k, :], [ABt[k]]) for k in range(16)])
          for m in range(16):
              w3, wt = load_w(KC, 256, [("w_in", 0, 5120 + m * 128, 128), ("w_in", 0, 7168 + m * 128, 128)])

              def ev_g(mi, tb, pap, pt, m=m):
                  dst = SF if mi == 0 else RS
                  dt_ = SFh if mi == 0 else [RSt]
                  act(op("activation", out=dst[:, tb * TB:(tb + 1) * TB], in_=pap, func=AF.Sigmoid), [pt], dt_)
              proj(w3, wt, KC, [0, 128], lambda k, tb: U[:, k, tb * TB:(tb + 1) * TB], lambda k: [Ut[k]], ev_g)
              w3, wt = load_w(8, 256, [("w_bp", 0, m * 128, 128), ("w_bh", 0, m * 128, 128)])
              st = P.pa_set
              P.pa_set ^= 1
              for k in range(8):
                  for tb in range(NB):
                      mm(PA[:, st * 2 + tb, :], w3[:, k, 0:128], AB[:, k, tb * TB:(tb + 1) * TB], k == 0, k == 7,
                         reads=[wt, ABt[k]], writes=[PAt[st * 2 + tb]], signal=(k == 7))
              for k in range(8):
                  for tb in range(NB):
                      mm(PM[:, 2 + tb, :], w3[:, k, 128:256], AB[:, 8 + k, tb * TB:(tb + 1) * TB], k == 0, k == 7,
                         reads=[wt, ABt[8 + k]], writes=[PMt[2 + tb]], signal=(k == 7))
              for tb in range(NB):
                  sl = slice(tb * TB, (tb + 1) * TB)
                  dve(op("tensor_tensor", out=SF[:, sl], in0=PA[:, st * 2 + tb, :], in1=SF[:, sl], op=ALU.mult),
                      [PAt[st * 2 + tb]] + SFh, SFh)
                  dve(op("tensor_tensor", out=RS[:, sl], in0=PM[:, 2 + tb, :], in1=RS[:, sl], op=ALU.mult),
                      [PMt[2 + tb], RSt], [RSt])
                  dve(op("tensor_tensor", out=X[:, m, sl], in0=SF[:, sl], in1=RS[:, sl], op=ALU.add),
                      SFh + [RSt], [Xt[m]])
          dump("MG", [(X[:, k, :], [Xt[k]]) for k in range(16)])
          for mg in range(8):
              w3, wt = load_w(KC, 256, [("w_mix", 0, mg * 256, 256)])
              proj(w3, wt, KC, [0, 128], lambda k, tb: X[:, k, tb * TB:(tb + 1) * TB], lambda k: [Xt[k]], evac_resid(mg * 2))
          dump("H1", [])
          norm_stats()
          MN = Xb(4, 4).rearrange("p (k m) -> p k m", m=MEM)
          tMN = Xt[4:8]
          KTm = Xb(8, 4).rearrange("p (k m) -> p k m", m=MEM)
          tKT = Xt[8:12]
          VTM = Xb(12, 4).rearrange("p (j d) -> p j d", d=D)
          tVT = Xt[12:16]
          for k in range(KC):
              dve(op("tensor_scalar", out=MN[:, k, :], in0=MEMH[:, k, :], scalar1=PAR[:, pb + P_MEM + k:pb + P_MEM + k + 1],
                                                 scalar2=None, op0=ALU.mult), [MEMHt, PARt], tMN)
          def kv_proj(hd):
              for dg in (2 * hd, 2 * hd + 1):
                  w3, wt = load_w(KC, 256, [("w_xkv", 0, dg * 256, 256)])

                  def ev_k(mi, tb, pap, pt, dg=dg):
                      act(op("activation", out=KTm[:, dg * 2 + mi, :], in_=pap, func=AF.Copy), [pt], tKT)
                  proj(w3, wt, KC, [0, 128], lambda k, tb: MN[:, k, :], lambda k: tMN, ev_k, n=MEM, ntb=1)
              for dg in (2 * hd, 2 * hd + 1):
                  w3, wt = load_w(KC, 256, [("w_xkv", 0, 2048 + dg * 256, 256)])
                  st = P.pa_set
                  P.pa_set ^= 1
                  for k in range(KC):
                      for mt in range(2):
                          mm(PA[:, st * 2 + mt, 0:256], MN[:, k, mt * 128:(mt + 1) * 128], w3[:, k, :], k == 0, k == KC - 1,
                             reads=[wt] + tMN, writes=[PAt[st * 2 + mt]], signal=(k == KC - 1))
                  for mt in range(2):
                      act(op("activation", out=VTM[:, mt, dg * 256:(dg + 1) * 256], in_=PA[:, st * 2 + mt, 0:256], func=AF.Copy),
                          [PAt[st * 2 + mt]], tVT)
          kv_proj(0)
          norm_apply(pb + P_CROSS)
          for hd in range(4):
              if hd > 0:
                  kv_proj(hd)
              for qg in range(2):
                  w3, wt = load_w(KC, 256, [("w_xq", 0, hd * 512 + qg * 256, 256)])

                  def ev_q(mi, tb, pap, pt, qg=qg):
                      ch = qg * 2 + mi
                      act(op("activation", out=X[:, ch, tb * TB:(tb + 1) * TB], in_=pap, func=AF.Copy, scale=512.0 ** -0.5),
                          [pt], [Xt[ch]])
                  proj(w3, wt, KC, [0, 128], lambda k, tb: U[:, k, tb * TB:(tb + 1) * TB], lambda k: [Ut[k]], ev_q)
              SFb = SF[:, :].bitcast(BF16)
              PTs = [SFb[:, tb * 1024:(tb + 1) * 1024].rearrange("p (j t) -> p j t", t=TB) for tb in range(NB)]
              RDs = [RS[:, tb * TB:(tb + 1) * TB] for tb in range(NB)]
              for tb in range(NB):
                  sl = slice(tb * TB, (tb + 1) * TB)
                  for mt in range(2):
                      bk = 2 * tb + mt
                      for dc in range(4):
                          mm(PM[:, bk, :], KTm[:, hd * 4 + dc, mt * 128:(mt + 1) * 128], X[:, dc, sl], dc == 0, dc == 3,
                             reads=tKT + [Xt[dc]], writes=[PMt[bk]], signal=(dc == 3))
                      act(op("activation", out=PTs[tb][:, mt, :], in_=PM[:, bk, :], func=AF.Exp), [PMt[bk]], [SFh[tb]])
              for tb in range(NB):
                  for mt in range(2):
                      mm(PA[:, tb, :], ONES[:, :], PTs[tb][:, mt, :], mt == 0, mt == 1, reads=[SFh[tb], CSTt], writes=[PAt[tb]],
                         signal=(mt == 1))
                  act(op("activation", out=PA[:, tb, :], in_=PA[:, tb, :], func=AF.Ln), [PAt[tb]], [PAt[tb]])
                  act(op("activation", out=RDs[tb], in_=PA[:, tb, :], func=AF.Exp, scale=-1.0), [PAt[tb]], [RSt])
              for tb in range(NB):
                  sl = slice(tb * TB, (tb + 1) * TB)
                  for dc in range(4):
                      ch = hd * 4 + dc
                      ob = 2 + dc % 2
                      for mt in range(2):
                          mm(PA[:, ob, :], VTM[:, mt, ch * 128:(ch + 1) * 128], PTs[tb][:, mt, :], mt == 0, mt == 1,
                             reads=tVT + [SFh[tb]], writes=[PAt[ob]], signal=(mt == 1))
                      dve(op("tensor_tensor", out=AB[:, ch, sl], in0=PA[:, ob, :], in1=RDs[tb], op=ALU.mult),
                          [PAt[ob], RSt], [ABt[ch]])
          for mg in range(8):
              w3, wt = load_w(KC, 256, [("w_xo", 0, mg * 256, 256)])
              proj(w3, wt, KC, [0, 128], lambda k, tb: AB[:, k, tb * TB:(tb + 1) * TB], lambda k: [ABt[k]], evac_resid(mg * 2))
          dump("H2", [])
          rmsnorm_U(pb + P_FFN)
          f0 = 0
          for bi, nb_ in enumerate(FFB):
              ABUF, tAB_ = (AB, ABt) if bi % 2 == 0 else (X, Xt)
              for i in range(nb_):
                  fc = f0 + i
                  w3, wt = load_w(KC, 256, [("w_ffi", 0, fc * 128, 128), ("w_ffi", 0, DFF + fc * 128, 128)])
                  st = P.pa_set
                  P.pa_set ^= 1
                  for k in range(KC):
                      for tb in range(NB):
                          mm(PA[:, st * 2 + tb, :], w3[:, k, 0:128], U[:, k, tb * TB:(tb + 1) * TB], k == 0, k == KC - 1,
                             reads=[wt, Ut[k]], writes=[PAt[st * 2 + tb]], signal=(k == KC - 1))
                  for k in range(KC):
                      for tb in range(NB):
                          mm(PM[:, 2 + tb, :], w3[:, k, 128:256], U[:, k, tb * TB:(tb + 1) * TB], k == 0, k == KC - 1,
                             reads=[wt, Ut[k]], writes=[PMt[2 + tb]], signal=(k == KC - 1))
                  for tb in range(NB):
                      sl = slice(tb * TB, (tb + 1) * TB)
                      act(op("activation", out=RS[:, sl], in_=PA[:, st * 2 + tb, :], func=AF.Silu),
                          [PAt[st * 2 + tb]], [RSt])
                      dve(op("tensor_tensor", out=ABUF[:, i, sl], in0=PM[:, 2 + tb, :], in1=RS[:, sl],
                                                                                  op=ALU.mult), [PMt[2 + tb], RSt], [tAB_[i]])
              for mg in range(8):
                  w3, wt = load_w(nb_, 256, [("w_ffo", f0 * 128, mg * 256, 256)])
                  proj(w3, wt, nb_, [0, 128], lambda k, tb, ABUF=ABUF: ABUF[:, k, tb * TB:(tb + 1) * TB],
                       lambda k, tAB_=tAB_: [tAB_[k]], evac_resid(mg * 2))
              f0 += nb_

    except StopBuild:
        final_norm = False
    if final_norm:
        norm_stats()
        norm_apply(P_FINAL, dst=H)
    last = None
    for k in range(KC):
        last = P.dma("sp", "d:out", op("dma_start", out=d_out[:, k, :], in_=H[:, k, :]), reads=[Ht[k]])
    P._deps("sp", [], [], [last])
    wpp = max([sp[1] + sp[2] * sp[3] for sp in P.wspecs] + [128])
    P.wpp = wpp
    d_w = nc.dram_tensor("wts", [nl, 128, wpp], F32, kind="ExternalInput").ap()
    wsrc.append(d_w)
    with nc.Block() as block:
        @block.tensor
        def _(e):
            for fn in P.q["pe"]:
                fn(e)

        @block.scalar
        def _(e):
            for fn in P.q["act"]:
                fn(e)

        @block.vector
        def _(e):
            for fn in P.q["dve"]:
                fn(e)

        @block.gpsimd
        def _(e):
            for fn in P.q["pool"]:
                fn(e)

        @block.sync
        def _(e):
            for fn in P.q["sp"]:
                fn(e)
    es.close()
    return nc, P


def _fm(a):
    t, dd = a.shape
    return np.ascontiguousarray(a.reshape(t, dd // 128, 128).transpose(2, 1, 0))


def _cols(v):
    return v.reshape(-1, 128).T


def _params(inp, half):
    par = np.zeros((128, NPC), np.float32)
    for l in range(DEPTH):
        b = l * PLC
        par[:, b + P_MIX:b + P_MIX + 16] = _cols(inp["norm_mix"][l])
        par[:, b + P_MEM:b + P_MEM + 16] = _cols(inp["norm_mem"][l])
        par[:, b + P_CROSS:b + P_CROSS + 16] = _cols(inp["norm_cross"][l])
        par[:, b + P_FFN:b + P_FFN + 16] = _cols(inp["norm_ffn"][l])
        par[:, b + P_PSC:b + P_PSC + 8] = _cols(inp["pool_scale"][l])
        par[:, b + P_HN:b + P_HN + 8] = _cols(inp["hgrn_norm"][l])
        par[:, b + P_LB:b + P_LB + 8] = _cols(inp["hgrn_lower_bounds"][l])
    par[:, P_FINAL:P_FINAL + 16] = _cols(inp["norm_final"])
    par[:, P_FLAG] = float(half)
    for g, w in enumerate(WINDOWS):
        for t in range(16):
            cnt = w if half == 1 else min(t + 1, w)
            par[:, P_INVC + g * 16 + t] = np.float32(1.0) / np.float32(cnt)
    return par


def _consts():
    c = np.zeros((128, 128 + 512 + 1024), np.float32)
    c[:, 0:128] = np.eye(128, dtype=np.float32)
    p = np.arange(128)[:, None] % 64
    t = np.arange(512)[None, :] % 64
    c[:, 128:640] = (p <= t).astype(np.float32)
    c[:, 640:1664] = (np.arange(1024)[None, :] % 64 != 0).astype(np.float32)
    return c


_WNAMES = {"w_in": "w_in", "w_pool": "w_pool_group", "w_bp": "w_branch_pool", "w_bh": "w_branch_hgrn", "w_mix": "w_mix_out",
           "w_xq": "w_xq", "w_xkv": "w_xkv", "w_xo": "w_xo", "w_ffi": "w_ffn_in", "w_ffo": "w_ffn_out"}


def _pack_weights(P, inp, layers):
    nl = len(layers)
    out = np.zeros((nl, 128, P.wpp), np.float32)
    for (li, off, kc, ncols, srcs) in P.wspecs:
        l = layers[li]
        c = off
        blk = out[li, :, off:off + kc * ncols].reshape(128, kc, ncols)
        cc = 0
        for (wn, r0, c0, nci) in srcs:
            W = inp[_WNAMES[wn]][l]
            if wn == "w_pool":
                W = W.reshape(1024, 256)
            blk[:, :, cc:cc + nci] = W[r0:r0 + kc * 128, c0:c0 + nci].reshape(kc, 128, nci).transpose(1, 0, 2)
            cc += nci
    return out


_CACHE = {}


def _get_prog(layers, final_norm, ncores, dbg=None):
    key = (tuple(layers), final_norm, ncores, dbg)
    if key not in _CACHE:
        _CACHE[key] = build(list(layers), final_norm, ncores, dbg)
    return _CACHE[key]


def run_launch(inp, h_fm, layers, final_norm, ncores=8, dbg=None):
    nc, P = _get_prog(layers, final_norm, ncores, dbg)
    wts = _pack_weights(P, inp, layers)
    cst = _consts()
    in_maps = []
    for c in range(ncores):
        b, half = c // 2, c % 2
        in_maps.append({"hin": h_fm[c], "memt": _fm(np.asarray(inp["mem"][b])), "par": _params(inp, half), "cst": cst, "wts": wts})
    res = run_bass_kernel_spmd(nc, in_maps, core_ids=list(range(ncores)))
    return [np.asarray(r["hout"]) for r in res.results]


FUSED = True


def kernel(**inputs):
    inp = {k: np.asarray(v) for k, v in inputs.items()}
    x = inp["x"]
    ncores = 8
    h = [_fm(x[c // 2, (c % 2) * NT:(c % 2 + 1) * NT, :]) for c in range(ncores)]
    if FUSED:
        h = run_launch(inp, h, list(range(DEPTH)), True, ncores)
    else:
        for l in range(DEPTH):
            h = run_launch(inp, h, [l], l == DEPTH - 1, ncores)
    out = np.zeros(x.shape, np.float32)
    for c in range(ncores):
        out[c // 2, (c % 2) * NT:(c % 2 + 1) * NT, :] = h[c].transpose(2, 1, 0).reshape(NT, D)
    return out
```

```python
import os
from contextlib import ExitStack
import numpy as np
import concourse.bass as bass
import concourse.mybir as mybir
from concourse.bass_utils import run_bass_kernel_spmd

F32 = mybir.dt.float32
BF16 = mybir.dt.bfloat16
AF = mybir.ActivationFunctionType
ALU = mybir.AluOpType
RSQ = AF.Abs_reciprocal_sqrt

D = 2048
NT = 1024
NB = 2
TB = 512
MEM = 256
DFF = 5632
EPS = 1e-6
DEPTH = 4
KC = 16
PLC = 88
P_MIX, P_MEM, P_CROSS, P_FFN, P_PSC, P_HN, P_LB = 0, 16, 32, 48, 64, 72, 80
P_FINAL = DEPTH * PLC
P_FLAG = P_FINAL + 16
P_INVC = P_FLAG + 1
NPC = P_INVC + 64
WINDOWS = (2, 4, 8, 16)
WG = 4096
NWB = 3
SELF_SYNC = True
FFB = (12, 12, 12, 8)


class op:
    def __init__(self, name, *a, **k):
        self.name, self.a, self.k = name, a, k

    def __call__(self, e):
        return getattr(e, self.name)(*self.a, **self.k)


class T:
    __slots__ = ("w", "r")

    def __init__(self):
        self.w = {}
        self.r = {}


class Prog:
    def __init__(self, nc, layers, final_norm):
        self.nc = nc
        self.layers = layers
        self.final_norm = final_norm
        self.q = {e: [] for e in ("pe", "act", "dve", "pool", "sp")}
        self.cnt = {}
        self.waited = {e: {} for e in self.q}
        self.sems = {}
        self.pend = {e: ([], []) for e in self.q}
        self.wspecs = []
        self.woff = 0
        self.wi = 0
        self.pa_set = 0
        self.ncc = 0

    def sem(self, key):
        if key not in self.sems:
            self.sems[key] = self.nc.alloc_semaphore(name="s_" + key.replace(":", "_"))
            self.cnt[key] = 0
        return self.sems[key]

    def _deps(self, eng, reads, writes, extra):
        deps = {}
        for t in reads:
            for k, v in t.w.items():
                deps[k] = max(deps.get(k, 0), v)
        for t in writes:
            for k, v in t.w.items():
                deps[k] = max(deps.get(k, 0), v)
            for k, v in t.r.items():
                deps[k] = max(deps.get(k, 0), v)
        for k, v in extra:
            deps[k] = max(deps.get(k, 0), v)
        for k, v in deps.items():
            if k == eng and (eng == "pe" or not SELF_SYNC):
                continue
            if self.waited[eng].get(k, 0) >= v:
                continue
            self.waited[eng][k] = v
            s = self.sem(k)
            self.q[eng].append(op("wait_ge", s, v))

    def emit(self, eng, fn, reads=(), writes=(), extra=(), signal=True):
        self._deps(eng, reads, writes, extra)
        pr, pw = self.pend[eng]
        pr.extend(reads)
        pw.extend(writes)
        if not signal:
            self.q[eng].append(lambda e, fn=fn: fn(e))
            return None
        s = self.sem(eng)
        self.cnt[eng] += 1
        n = self.cnt[eng]
        self.q[eng].append(lambda e, fn=fn, s=s: fn(e).then_inc(s, 1))
        for t in pr:
            t.r[eng] = n
        for t in pw:
            t.w[eng] = n
        self.pend[eng] = ([], [])
        return (eng, n)

    def dma(self, eng, key, fn, reads=(), writes=(), extra=()):
        self._deps(eng, reads, writes, extra)
        s = self.sem(key)
        self.cnt[key] += 16
        n = self.cnt[key]
        self.q[eng].append(lambda e, fn=fn, s=s: fn(e).then_inc(s, 16))
        for t in reads:
            t.r[key] = n
        for t in writes:
            t.w[key] = n
        return (key, n)


class StopBuild(Exception):
    pass


def build(layers, final_norm, ncores=8, dbg=None):
    nc = bass.Bass("TRN2", target_bir_lowering=False)
    P = Prog(nc, layers, final_norm)
    P.ncores = ncores
    nl = len(layers)
    es = ExitStack()

    def dram(name, shape, kind, dt=F32):
        return nc.dram_tensor(name, shape, dt, kind=kind).ap()

    def sb(name, shape, dt):
        return es.enter_context(nc.sbuf_tensor(name, shape, dt))

    def ps(name, shape, dt):
        return es.enter_context(nc.psum_tensor(name, shape, dt))

    d_h = dram("hin", [128, KC, NT], "ExternalInput")
    d_mem = dram("memt", [128, KC, MEM], "ExternalInput")
    d_par = dram("par", [128, NPC], "ExternalInput")
    d_cst = dram("cst", [128, 128 + 512 + 1024], "ExternalInput")
    d_out = dram("hout", [128, KC, NT], "ExternalOutput")
    d_w = None
    cc_inA = [dram(f"ccinA{i}", [128, 640], "Internal") for i in range(nl)]
    cc_outA = [dram(f"ccoutA{i}", [256, 640], "Internal") for i in range(nl)]
    cc_inB = [dram(f"ccinB{i}", [128, 512], "Internal") for i in range(nl)]
    cc_outB = [dram(f"ccoutB{i}", [256, 512], "Internal") for i in range(nl)]

    H = sb("H", [128, KC, NT], F32)
    U = sb("U", [128, KC, NT], BF16)
    AB = sb("AB", [128, 16, NT], BF16)
    X = sb("X", [128, 16, NT], BF16)
    WB = [sb(f"W{i}", [128, WG], BF16) for i in range(NWB)]
    RS = sb("RS", [128, NT], F32)
    PAR = sb("PAR", [128, NPC], F32)
    MEMH = sb("MEMH", [128, KC, MEM], BF16)
    IDENT = sb("IDENT", [128, 128], BF16)
    ONES = sb("ONES", [128, 128], BF16)
    TRI = sb("TRI", [128, 512], BF16)
    SMASK = sb("SMASK", [128, NT], BF16)
    PFX = sb("PFX", [128, 16], F32)
    SF = sb("SF", [128, 8 * 128], F32)
    ZH = sb("ZH", [128, 128], F32)
    TST = [sb(f"TST{i}", [128, 128], F32) for i in range(2)]
    DC = sb("DC", [128, 16], F32)
    GT = sb("GT", [128, 17], F32)
    ZER = sb("ZER", [128, 16], F32)
    LB = sb("LB", [128, 4, 8], F32)
    OML = sb("OML", [128, 4, 8], F32)
    LBE = sb("LBE", [128, 4, 8], F32)
    LBT = sb("LBT", [128, 3, 8], F32)
    PA = ps("PA", [128, 4, 512], F32)
    PM = ps("PM", [128, 4, 512], F32)

    Ht = [T() for _ in range(KC)]
    Ut = [T() for _ in range(KC)]
    ABt = [T() for _ in range(16)]
    Xt = [T() for _ in range(16)]
    WBt = [T() for _ in range(NWB)]
    PAt = [T() for _ in range(4)]
    PMt = [T() for _ in range(4)]
    RSt, PARt, MEMHt, CSTt, ZHt, DCt, LBt_ = T(), T(), T(), T(), T(), T(), T()
    SFh = [T(), T()]
    TSTt = [T(), T()]
    GTt = T()
    PFXt = T()

    def Xb(j, n=1):
        return X[:, j:j + n, :].rearrange("p c t -> p (c t)")

    def Xf(j, n=2):
        return X[:, j:j + n, :].rearrange("p c t -> p (c t)").bitcast(F32)

    wsrc = []

    def load_w(kc, ncols, srcs):
        g = kc * ncols
        assert g <= WG and sum(s[3] for s in srcs) == ncols
        i = P.wi % NWB
        P.wi += 1
        off = P.woff
        P.woff += g
        buf = WB[i]
        li = P.cur_l
        P.wspecs.append((li, off, kc, ncols, srcs))

        def fn(e, buf=buf, off=off, g=g, li=li):
            return e.dma_start(out=buf[:, 0:g], in_=wsrc[0][li, :, off:off + g], max_dma_last_dim=4096)
        P.dma("pool", f"d:w{i}", fn, writes=[WBt[i]])
        return buf[:, 0:g].rearrange("p (k c) -> p k c", c=ncols), WBt[i]

    def act(fn, reads=(), writes=()):
        return P.emit("act", fn, reads, writes)

    def dve(fn, reads=(), writes=()):
        return P.emit("dve", fn, reads, writes)

    def mm(out, lhsT, rhs, start, stop, reads=(), writes=(), signal=False):
        return P.emit("pe", op("matmul", out, lhsT, rhs, start=start, stop=stop), reads, writes, signal=signal)

    def proj(w3, wt, kc, mcols, rhs_fn, rhs_tiles_fn, evac, n=TB, ntb=NB):
        for mi, c0 in enumerate(mcols):
            st = P.pa_set
            P.pa_set ^= 1
            for k in range(kc):
                for tb in range(ntb):
                    bank = st * 2 + tb
                    mm(PA[:, bank, 0:n], w3[:, k, c0:c0 + 128], rhs_fn(k, tb), k == 0, k == kc - 1,
                       reads=[wt] + rhs_tiles_fn(k), writes=[PAt[bank]], signal=(k == kc - 1))
            for tb in range(ntb):
                bank = st * 2 + tb
                evac(mi, tb, PA[:, bank, 0:n], PAt[bank])

    def norm_stats():
        for k in range(KC):
            j = k % 2
            act(op("activation", out=Xb(j), in_=H[:, k, :], func=AF.Square), [Ht[k]], [Xt[j]])
            for tb in range(NB):
                mm(PM[:, tb, :], ONES[:, :], Xb(j)[:, tb * TB:(tb + 1) * TB], k == 0, k == KC - 1,
                   reads=[Xt[j], CSTt], writes=[PMt[tb]], signal=True)
        for tb in range(NB):
            act(op("activation", out=PM[:, tb, :], in_=PM[:, tb, :], func=AF.Ln, scale=1.0 / D, bias=EPSC[:, 0:1]),
                [PMt[tb], CSTt], [PMt[tb]])
            act(op("activation", out=PM[:, tb, :], in_=PM[:, tb, :], func=AF.Exp, scale=-0.5), [PMt[tb]], [PMt[tb]])

    def norm_apply(gcol, dst=None):
        for k in range(KC):
            for tb in range(NB):
                sl = slice(tb * TB, (tb + 1) * TB)
                if dst is None:
                    dve(op("scalar_tensor_tensor", out=U[:, k, sl], in0=H[:, k, sl], scalar=PAR[:, gcol + k:gcol + k + 1],
                           in1=PM[:, tb, :], op0=ALU.mult, op1=ALU.mult), [Ht[k], PMt[tb], PARt], [Ut[k]])
                else:
                    dve(op("scalar_tensor_tensor", out=H[:, k, sl], in0=H[:, k, sl], scalar=PAR[:, gcol + k:gcol + k + 1],
                           in1=PM[:, tb, :], op0=ALU.mult, op1=ALU.mult), [Ht[k], PMt[tb], PARt], [Ht[k]])

    def rmsnorm_U(gcol):
        norm_stats()
        norm_apply(gcol)

    def evac_resid(m0):
        def f(mi, tb, pap, pt):
            m = m0 + mi
            dve(op("tensor_tensor", out=H[:, m, tb * TB:(tb + 1) * TB], in0=pap, in1=H[:, m, tb * TB:(tb + 1) * TB],
                                          op=ALU.add), [pt, Ht[m]], [Ht[m]])
        return f

    EPSC = sb("EPSC", [128, 2], F32)

    def dump(stage, items):
        if dbg != stage:
            return
        for k, (src, tl) in enumerate(items):
            n = src.shape[-1]
            act(op("activation", out=H[:, k, 0:n], in_=src, func=AF.Copy), list(tl), [Ht[k]])
        raise StopBuild()

    P.cur_l = 0
    P.dma("sp", "d:in", op("dma_start", out=PAR[:, :], in_=d_par[:, :]), writes=[PARt])
    for k in range(KC):
        P.dma("sp", "d:in", op("dma_start", out=H[:, k, :], in_=d_h[:, k, :]), writes=[Ht[k]])
    CF = Xf(0, 4)
    P.dma("sp", "d:in", op("dma_start", out=CF[:, 0:1664], in_=d_cst[:, :]), writes=Xt[0:4])
    MF = Xf(4, 8).rearrange("p (k m) -> p k m", m=MEM)
    P.dma("sp", "d:in", op("dma_start", out=MF, in_=d_mem[:, :, :]), writes=Xt[4:12])
    for t in [PARt] + Ht + Xt[0:12]:
        t.w["d:in"] = P.cnt["d:in"]
    dve(op("tensor_copy", out=IDENT[:, :], in_=CF[:, 0:128]), Xt[0:4], [CSTt])
    dve(op("tensor_copy", out=TRI[:, :], in_=CF[:, 128:640]), Xt[0:4], [CSTt])
    dve(op("tensor_copy", out=SMASK[:, :], in_=CF[:, 640:1664]), Xt[0:4], [CSTt])
    dve(op("memset", ONES[:, :], 1.0), [], [CSTt])
    dve(op("memset", EPSC[:, :], EPS), [], [CSTt])
    dve(op("memset", ZER[:, :], 0.0), [], [CSTt])
    dve(op("memset", GT[:, :], 1.0), [], [GTt])
    for l in range(4):
        act(op("activation", out=LBE[:, l, :], in_=PAR[:, l * PLC + P_LB:l * PLC + P_LB + 8], func=AF.Exp),
            [PARt], [LBt_])
    dve(op("tensor_tensor", out=LBT[:, 0, :], in0=LBE[:, 1, :], in1=LBE[:, 2, :], op=ALU.add), [LBt_], [LBt_])
    dve(op("tensor_tensor", out=LBT[:, 1, :], in0=LBT[:, 0, :], in1=LBE[:, 3, :], op=ALU.add), [LBt_], [LBt_])
    dve(op("tensor_tensor", out=LBT[:, 2, :], in0=LBT[:, 1, :], in1=LBE[:, 0, :], op=ALU.add), [LBt_], [LBt_])
    dve(op("reciprocal", out=LBT[:, 2, :], in_=LBT[:, 2, :]), [LBt_], [LBt_])
    dve(op("memset", LB[:, 0, :], 0.0), [], [LBt_])
    dve(op("tensor_tensor", out=LB[:, 1, :], in0=LBE[:, 1, :], in1=LBT[:, 2, :], op=ALU.mult), [LBt_], [LBt_])
    dve(op("tensor_tensor", out=LB[:, 2, :], in0=LBT[:, 0, :], in1=LBT[:, 2, :], op=ALU.mult), [LBt_], [LBt_])
    dve(op("tensor_tensor", out=LB[:, 3, :], in0=LBT[:, 1, :], in1=LBT[:, 2, :], op=ALU.mult), [LBt_], [LBt_])
    dve(op("tensor_scalar", out=OML[:, :, :], in0=LB[:, :, :], scalar1=-1.0, scalar2=1.0, op0=ALU.mult, op1=ALU.add),
        [LBt_], [LBt_])
    for k in range(KC):
        j = 12 + k % 2
        act(op("activation", out=Xb(j)[:, 0:MEM], in_=MF[:, k, :], func=AF.Square), Xt[4:12], [Xt[j]])
        mm(PM[:, 0, 0:MEM], ONES[:, :], Xb(j)[:, 0:MEM], k == 0, k == KC - 1, reads=[Xt[j], CSTt], writes=[PMt[0]], signal=True)
    act(op("activation", out=RS[:, 0:MEM], in_=PM[:, 0, 0:MEM], func=AF.Sqrt, scale=1.0 / D, bias=EPSC[:, 0:1]),
        [PMt[0], CSTt], [RSt])
    dve(op("reciprocal", out=RS[:, 0:MEM], in_=RS[:, 0:MEM]), [RSt], [RSt])
    for k in range(KC):
        dve(op("tensor_tensor", out=MEMH[:, k, :], in0=MF[:, k, :], in1=RS[:, 0:MEM], op=ALU.mult),
            Xt[4:12] + [RSt], [MEMHt])

    try:
      for li, l in enumerate(layers):
          P.cur_l = li
          P.woff = 0
          pb = l * PLC
          rmsnorm_U(pb + P_MIX)
          dump("U", [(U[:, k, :], [Ut[k]]) for k in range(KC)])
          FB, LG = Xf(0), Xf(2)
          tFB, tLG = Xt[0:2], Xt[2:4]
          VT, KTT, SC, SBF = Xb(10), Xb(11), Xb(12), Xb(13, 2)
          tSBF = Xt[13:15]

          def hset(h):
            s_ = h % 2
            return (Xb(4 + 3 * s_), Xb(5 + 3 * s_), Xb(6 + 3 * s_), Xt[4 + 3 * s_], Xt[5 + 3 * s_], Xt[6 + 3 * s_])

          def hg_proj(h):
            QT, KT, VF, tQT, tKT, tVF = hset(h)
            w3, wt = load_w(KC, 256, [("w_in", 0, 1024 + h * 128, 128), ("w_in", 0, 2048 + h * 128, 128)])

            def ev_qf(mi, tb, pap, pt):
                sl = slice(tb * TB, (tb + 1) * TB)
                if mi == 0:
                    act(op("activation", out=QT[:, sl], in_=pap, func=AF.Silu), [pt], [tQT])
                else:
                    act(op("activation", out=FB[:, sl], in_=pap, func=AF.Sigmoid), [pt], tFB)
            proj(w3, wt, KC, [0, 128], lambda k, tb: U[:, k, tb * TB:(tb + 1) * TB], lambda k: [Ut[k]], ev_qf)
            w3, wt = load_w(KC, 128, [("w_in", 0, 3072 + h * 128, 128)])

            def ev_v(mi, tb, pap, pt):
                sl = slice(tb * TB, (tb + 1) * TB)
                act(op("activation", out=VF[:, sl], in_=pap, func=AF.Copy), [pt], [tVF])
            proj(w3, wt, KC, [0], lambda k, tb: U[:, k, tb * TB:(tb + 1) * TB], lambda k: [Ut[k]], ev_v)

          def hg_stage1(h):
            QT, KT, VF, tQT, tKT, tVF = hset(h)
            lbc = LB[:, l, h:h + 1]
            omlc = OML[:, l, h:h + 1]
            dve(op("tensor_scalar", out=FB, in0=FB, scalar1=omlc, scalar2=lbc, op0=ALU.mult, op1=ALU.add), tFB + [LBt_], tFB)
            act(op("activation", out=LG, in_=FB, func=AF.Ln), tFB, tLG)
            dve(op("tensor_scalar", out=KT, in0=FB, scalar1=-1.0, scalar2=1.0, op0=ALU.mult, op1=ALU.add), tFB, [tKT])
            dve(op("tensor_tensor_scan", out=FB, data0=SMASK[:, :], data1=LG, initial=0.0, op0=ALU.mult, op1=ALU.add),
                tLG + [CSTt], tFB)
            act(op("activation", out=DC[:, :], in_=FB.rearrange("p (c t) -> p c t", t=64)[:, :, 63], func=AF.Exp), tFB, [DCt])
            act(op("activation", out=LG, in_=FB, func=AF.Exp), tFB, tLG)
            act(op("activation", out=FB, in_=FB, func=AF.Exp, scale=-1.0), tFB, tFB)
            dve(op("tensor_tensor_scan", out=GT[:, 1:17], data0=DC[:, :], data1=ZER[:, :], initial=1.0, op0=ALU.mult, op1=ALU.add),
                [DCt, CSTt], [GTt])
            dve(op("tensor_tensor", out=QT, in0=QT, in1=LG, op=ALU.mult), [tQT] + tLG, [tQT])
            dve(op("tensor_tensor", out=KT, in0=KT, in1=FB, op=ALU.mult), [tKT] + tFB, [tKT])
            dve(op("tensor_tensor", out=AB[:, h, :].rearrange("p (c t) -> p c t", t=64), in0=QT.rearrange("p (c t) -> p c t", t=64),
                   in1=GT[:, 0:16].unsqueeze(2).broadcast_to([128, 16, 64]), op=ALU.mult), [tQT, GTt], [ABt[h]])

          def hg_stage2(h):
            QT, KT, VF, tQT, tKT, tVF = hset(h)
            PMb1 = PM[:, 1, :].bitcast(BF16)
            PMb2 = PM[:, 2, :].bitcast(BF16)
            for j in range(8):
                P.emit("pe", op("transpose", PMb1[:, j * 128:(j + 1) * 128], VF[:, j * 128:(j + 1) * 128], IDENT[:, :]),
                       [tVF, CSTt], [PMt[1]], signal=(j == 7))
            for j in range(8):
                P.emit("pe", op("transpose", PMb2[:, j * 128:(j + 1) * 128], KT[:, j * 128:(j + 1) * 128], IDENT[:, :]),
                       [tKT, CSTt], [PMt[2]], signal=(j == 7))
            act(op("activation", out=VT, in_=PMb1, func=AF.Copy), [PMt[1]], [Xt[10]])
            dve(op("tensor_copy", out=KTT, in_=PMb2), [PMt[2]], [Xt[11]])
            for c in range(16):
                j, par = c // 2, c % 2
                ps_ = slice(par * 64, par * 64 + 64)
                mm(PM[ps_, 0, j * 64:(j + 1) * 64], KT[:, c * 64:(c + 1) * 64], QT[:, c * 64:(c + 1) * 64], True, True,
                   reads=[tQT, tKT], writes=[PMt[0]], signal=(c == 15))
            dve(op("tensor_tensor", out=SC[:, 0:512], in0=PM[:, 0, :], in1=TRI[:, :], op=ALU.mult), [PMt[0], CSTt], [Xt[12]])
            def o_chunk(c):
                j, par = c // 2, c % 2
                ps_ = slice(par * 64, par * 64 + 64)
                tb, cc = c // 8, c % 8
                mm(PM[:, 1 + tb, cc * 64:(cc + 1) * 64], VT[ps_, j * 128:(j + 1) * 128], SC[ps_, j * 64:(j + 1) * 64], True, c == 0,
                   reads=[Xt[10], Xt[12]], writes=[PMt[1 + tb]], signal=(c == 0))
                if c > 0:
                    mm(PM[:, 1 + tb, cc * 64:(cc + 1) * 64], SBF[:, c * 128:(c + 1) * 128], QT[:, c * 64:(c + 1) * 64], False, True,
                       reads=tSBF + [tQT], writes=[PMt[1 + tb]], signal=True)
            LAG = 16
            for c in range(16):
                j, par = c // 2, c % 2
                ps_ = slice(par * 64, par * 64 + 64)
                slot = c % 4
                mm(PM[:, 3, slot * 128:(slot + 1) * 128], KTT[ps_, j * 128:(j + 1) * 128], VT[ps_, j * 128:(j + 1) * 128], True, True,
                   reads=[Xt[10], Xt[11]], writes=[PMt[3]], signal=True)
                cur, prv = TST[c % 2], TST[(c + 1) % 2]
                if c == 0:
                    dve(op("tensor_copy", out=cur[:, :], in_=PM[:, 3, slot * 128:(slot + 1) * 128]), [PMt[3]], [TSTt[c % 2]])
                else:
                    dve(op("scalar_tensor_tensor", out=cur[:, :], in0=prv[:, :], scalar=DC[:, c - 1:c],
                           in1=PM[:, 3, slot * 128:(slot + 1) * 128], op0=ALU.mult, op1=ALU.add),
                        [PMt[3], TSTt[(c + 1) % 2], DCt], [TSTt[c % 2]])
                if c < 15:
                    act(op("activation", out=SBF[:, (c + 1) * 128:(c + 2) * 128], in_=cur[:, :], func=AF.Copy, scale=DC[:, c:c + 1]),
                        [TSTt[c % 2], DCt], tSBF)
                else:
                    act(op("activation", out=SF[:, h * 128:(h + 1) * 128], in_=cur[:, :], func=AF.Copy, scale=DC[:, c:c + 1]),
                        [TSTt[c % 2], DCt], [SFh[h // 4]])
                if c >= LAG:
                    o_chunk(c - LAG)
            for c in range(16 - LAG, 16):
                o_chunk(c)
            for tb in range(NB):
                act(op("activation", out=AB[:, 8 + h, tb * TB:(tb + 1) * TB], in_=PM[:, 1 + tb, :], func=AF.Copy),
                    [PMt[1 + tb]], [ABt[8 + h]])

          groups = [[2 * i, 2 * i + 1] for i in range(P.ncores // 2)]
          scc = P.sem("cc")

          def exchange(key, cin, cout, pieces):
              tk = None
              for (apx, tl, c0, ncl) in pieces:
                  tk = P.dma("sp", key, op("dma_start", out=cin[:, c0:c0 + ncl], in_=apx), reads=[tl])
              P._deps("pool", [], [], [tk])
              P.ncc += 1
              ncc = P.ncc
              ccop = op("collective_compute", "AllGather", ALU.bypass, replica_groups=groups, ins=[cin], outs=[cout])
              P.q["pool"].append(lambda e, ccop=ccop, scc=scc: ccop(e).then_inc(scc))
              P.cnt["cc"] = ncc
              for (apx, tl, c0, ncl) in pieces:
                  P.dma("sp", key, op("dma_start", out=apx, in_=cout[0:128, c0:c0 + ncl]), writes=[tl], extra=[("cc", ncc)])
              for (apx, tl, c0, ncl) in pieces:
                  tl.w[key] = P.cnt[key]

          def halo(c):
              w3, wt = load_w(KC, 128, [("w_in", 0, c * 128, 128)])

              def ev_h(mi, tb, pap, pt, c=c):
                  act(op("activation", out=ZH[:, c * 16:(c + 1) * 16], in_=pap, func=AF.Copy), [pt], [ZHt])
              proj(w3, wt, KC, [0], lambda k, tb: U[:, k, NT - 16:NT], lambda k: [Ut[k]], ev_h, n=16, ntb=1)

          hg_proj(0)
          for h in range(8):
              hg_stage1(h)
              if h < 7:
                  hg_proj(h + 1)
              if h < 4:
                  halo(2 * h)
                  halo(2 * h + 1)
              hg_stage2(h)
              if h == 3:
                  exchange("d:exA", cc_inA[li], cc_outA[li], [(SF[:, 0:512], SFh[0], 0, 512), (ZH[:, :], ZHt, 512, 128)])
          exchange("d:exB", cc_inB[li], cc_outB[li], [(SF[:, 512:1024], SFh[1], 0, 512)])
          dump("EX", [(SF[:, :], SFh)] + [(AB[:, 8 + k, :], [ABt[8 + k]]) for k in range(8)] + [(AB[:, k, :], [ABt[k]]) for k in range(7)])
          flag = PAR[:, P_FLAG:P_FLAG + 1]
          SIN = Xb(0)
          dve(op("tensor_scalar", out=SIN[:, 0:512], in0=SF[:, 0:512], scalar1=flag, scalar2=None, op0=ALU.mult), [SFh[0], PARt], [Xt[0]])
          dve(op("tensor_scalar", out=ZH[:, :], in0=ZH[:, :], scalar1=flag, scalar2=None, op0=ALU.mult), [ZHt, PARt], [ZHt])
          XFL = X[:, :, :].rearrange("p c t -> p (c t)")
          ZP = XFL[:, 1024:3104].bitcast(F32)
          T1 = XFL[:, 3104:5184].bitcast(F32)
          T2 = XFL[:, 5184:7264].bitcast(F32)
          tZP, tT1, tT2 = Xt[1:4], Xt[3:6], Xt[5:8]
          OGB = X[:, 8:10, :]
          OGt = [Xt[8], Xt[9]]
          OT, OSQ = Xf(12), Xb(14)
          tOT = Xt[12:14]
          n = NT + 16

          def pe_proj(c):
              w3, wt = load_w(KC, 256, [("w_in", 0, c * 128, 128), ("w_in", 0, 4096 + c * 128, 128)])

              def ev_po(mi, tb, pap, pt, c=c):
                  if mi == 0:
                      act(op("activation", out=ZP[:, 16 + tb * TB:16 + (tb + 1) * TB], in_=pap, func=AF.Copy), [pt], tZP)
                  else:
                      act(op("activation", out=OGB[:, c % 2, tb * TB:(tb + 1) * TB], in_=pap, func=AF.Silu), [pt], [OGt[c % 2]])
              proj(w3, wt, KC, [0, 128], lambda k, tb: U[:, k, tb * TB:(tb + 1) * TB], lambda k: [Ut[k]], ev_po)

          def corr_a(c):
              for tb in range(NB):
                  mm(PM[:, tb, :], SIN[:, c * 128:(c + 1) * 128], AB[:, c, tb * TB:(tb + 1) * TB], True, True,
                     reads=[Xt[0], ABt[c]], writes=[PMt[tb]], signal=True)
                  dve(op("tensor_tensor", out=OT[:, tb * TB:(tb + 1) * TB], in0=PM[:, tb, :], in1=AB[:, 8 + c, tb * TB:(tb + 1) * TB],
                         op=ALU.add), [PMt[tb], ABt[8 + c]], tOT)
              act(op("activation", out=OSQ, in_=OT, func=AF.Square), tOT, [Xt[14]])

          def corr_b(c):
              for tb in range(NB):
                  mm(PM[:, 2 + tb, :], ONES[:, :], OSQ[:, tb * TB:(tb + 1) * TB], True, True, reads=[Xt[14], CSTt], writes=[PMt[2 + tb]],
                     signal=True)
                  act(op("activation", out=PM[:, 2 + tb, :], in_=PM[:, 2 + tb, :], func=AF.Ln, scale=1.0 / 128, bias=EPSC[:, 0:1]),
                      [PMt[2 + tb], CSTt], [PMt[2 + tb]])
                  act(op("activation", out=PM[:, 2 + tb, :], in_=PM[:, 2 + tb, :], func=AF.Exp, scale=-0.5), [PMt[2 + tb]], [PMt[2 + tb]])
              for tb in range(NB):
                  sl = slice(tb * TB, (tb + 1) * TB)
                  dve(op("scalar_tensor_tensor", out=OT[:, sl], in0=OT[:, sl], scalar=PAR[:, pb + P_HN + c:pb + P_HN + c + 1],
                         in1=PM[:, 2 + tb, :], op0=ALU.mult, op1=ALU.mult), tOT + [PMt[2 + tb], PARt], tOT)
              dve(op("tensor_tensor", out=AB[:, 8 + c, :], in0=OT, in1=OGB[:, c % 2, :], op=ALU.mult), tOT + [OGt[c % 2]], [ABt[8 + c]])

          def pool_dve(c):
              g = c // 2
              wlen = WINDOWS[g]
              dve(op("tensor_copy", out=ZP[:, 0:16], in_=ZH[:, c * 16:(c + 1) * 16]), [ZHt], tZP)
              src, tsrc = ZP, tZP
              dsts = [(T1, tT1), (T2, tT2)]
              sh = 1
              di = 0
              while sh < wlen:
                  dst, tdst = dsts[di]
                  di ^= 1
                  lo = 2 * sh - 1
                  dve(op("tensor_tensor", out=dst[:, lo:n], in0=src[:, lo:n], in1=src[:, lo - sh:n - sh], op=ALU.add), tsrc, tdst)
                  src, tsrc = dst, tdst
                  sh *= 2
              PLb = Xb(10 + (c % 2))
              dve(op("scalar_tensor_tensor", out=PLb[:, :], in0=src[:, 16:n], scalar=1.0 / wlen, in1=ZP[:, 16:n], op0=ALU.mult,
                     op1=ALU.subtract), tsrc + tZP, [Xt[10 + (c % 2)]])
              dve(op("tensor_tensor", out=PFX[:, :], in0=src[:, 16:32], in1=PAR[:, P_INVC + g * 16:P_INVC + g * 16 + 16], op=ALU.mult),
                  tsrc + [PARt, PFXt], [PFXt])
              dve(op("tensor_tensor", out=PLb[:, 0:16], in0=PFX[:, :], in1=ZP[:, 16:32], op=ALU.subtract), [PFXt] + tZP, [Xt[10 + (c % 2)]])

          def pool_mix(c):
              g = c // 2
              w3g, wtg = load_w(2, 256, [("w_pool", g * 256, 0, 256)])

              def ev_a(mi, tb, pap, pt, g=g):
                  ch = 2 * g + mi
                  act(op("activation", out=AB[:, ch, tb * TB:(tb + 1) * TB], in_=pap, func=AF.Copy,
                         scale=PAR[:, pb + P_PSC + ch:pb + P_PSC + ch + 1]), [pt, PARt], [ABt[ch]])
              proj(w3g, wtg, 2, [0, 128], lambda k, tb: Xb(10 + k)[:, tb * TB:(tb + 1) * TB], lambda k: [Xt[10 + k]], ev_a)

          pe_proj(0)
          for c in range(8):
              if c == 4:
                  dve(op("tensor_scalar", out=SIN[:, 512:1024], in0=SF[:, 512:1024], scalar1=flag, scalar2=None, op0=ALU.mult),
                      [SFh[1], PARt], [Xt[0]])
              corr_a(c)
              pool_dve(c)
              if c < 7:
                  pe_proj(c + 1)
              corr_b(c)
              if c % 2 == 1:
                  pool_mix(c)
          dump("AB", [(AB[:, k, :], [ABt[k]]) for k in range(16)])
          for m in range(16):
              w3, wt = load_w(KC, 256, [("w_in", 0, 5120 + m * 128, 128), ("w_in", 0, 7168 + m * 128, 128)])

              def ev_g(mi, tb, pap, pt, m=m):
                  dst = SF if mi == 0 else RS
                  dt_ = SFh if mi == 0 else [RSt]
                  act(op("activation", out=dst[:, tb * TB:(tb + 1) * TB], in_=pap, func=AF.Sigmoid), [pt], dt_)
              proj(w3, wt, KC, [0, 128], lambda k, tb: U[:, k, tb * TB:(tb + 1) * TB], lambda k: [Ut[k]], ev_g)
              w3, wt = load_w(8, 256, [("w_bp", 0, m * 128, 128), ("w_bh", 0, m * 128, 128)])
              st = P.pa_set
              P.pa_set ^= 1
              for k in range(8):
                  for tb in range(NB):
                      mm(PA[:, st * 2 + tb, :], w3[:, k, 0:128], AB[:, k, tb * TB:(tb + 1) * TB], k == 0, k == 7,
                         reads=[wt, ABt[k]], writes=[PAt[st * 2 + tb]], signal=(k == 7))
              for k in range(8):
                  for tb in range(NB):
                      mm(PM[:, 2 + tb, :], w3[:, k, 128:256], AB[:, 8 + k, tb * TB:(tb + 1) * TB], k == 0, k == 7,
                         reads=[wt, ABt[8 + k]], writes=[PMt[2 + tb]], signal=(k == 7))
              for tb in range(NB):
                  sl = slice(tb * TB, (tb + 1) * TB)
                  dve(op("tensor_tensor", out=SF[:, sl], in0=PA[:, st * 2 + tb, :], in1=SF[:, sl], op=ALU.mult),
                      [PAt[st * 2 + tb]] + SFh, SFh)
                  dve(op("tensor_tensor", out=RS[:, sl], in0=PM[:, 2 + tb, :], in1=RS[:, sl], op=ALU.mult),
                      [PMt[2 + tb], RSt], [RSt])
                  dve(op("tensor_tensor", out=X[:, m, sl], in0=SF[:, sl], in1=RS[:, sl], op=ALU.add),
                      SFh + [RSt], [Xt[m]])
          dump("MG", [(X[:, k, :], [Xt[k]]) for k in range(16)])
          for mg in range(8):
              w3, wt = load_w(KC, 256, [("w_mix", 0, mg * 256, 256)])
              proj(w3, wt, KC, [0, 128], lambda k, tb: X[:, k, tb * TB:(tb + 1) * TB], lambda k: [Xt[k]], evac_resid(mg * 2))
          dump("H1", [])
          norm_stats()
          MN = Xb(4, 4).rearrange("p (k m) -> p k m", m=MEM)
          tMN = Xt[4:8]
          KTm = Xb(8, 4).rearrange("p (k m) -> p k m", m=MEM)
          tKT = Xt[8:12]
          VTM = Xb(12, 4).rearrange("p (j d) -> p j d", d=D)
          tVT = Xt[12:16]
          for k in range(KC):
              dve(op("tensor_scalar", out=MN[:, k, :], in0=MEMH[:, k, :], scalar1=PAR[:, pb + P_MEM + k:pb + P_MEM + k + 1],
                                                 scalar2=None, op0=ALU.mult), [MEMHt, PARt], tMN)
          def kv_proj(hd):
              for dg in (2 * hd, 2 * hd + 1):
                  w3, wt = load_w(KC, 256, [("w_xkv", 0, dg * 256, 256)])

                  def ev_k(mi, tb, pap, pt, dg=dg):
                      act(op("activation", out=KTm[:, dg * 2 + mi, :], in_=pap, func=AF.Copy), [pt], tKT)
                  proj(w3, wt, KC, [0, 128], lambda k, tb: MN[:, k, :], lambda k: tMN, ev_k, n=MEM, ntb=1)
              for dg in (2 * hd, 2 * hd + 1):
                  w3, wt = load_w(KC, 256, [("w_xkv", 0, 2048 + dg * 256, 256)])
                  st = P.pa_set
                  P.pa_set ^= 1
                  for k in range(KC):
                      for mt in range(2):
                          mm(PA[:, st * 2 + mt, 0:256], MN[:, k, mt * 128:(mt + 1) * 128], w3[:, k, :], k == 0, k == KC - 1,
                             reads=[wt] + tMN, writes=[PAt[st * 2 + mt]], signal=(k == KC - 1))
                  for mt in range(2):
                      act(op("activation", out=VTM[:, mt, dg * 256:(dg + 1) * 256], in_=PA[:, st * 2 + mt, 0:256], func=AF.Copy),
                          [PAt[st * 2 + mt]], tVT)
          kv_proj(0)
          norm_apply(pb + P_CROSS)
          for hd in range(4):
              if hd > 0:
                  kv_proj(hd)
              for qg in range(2):
                  w3, wt = load_w(KC, 256, [("w_xq", 0, hd * 512 + qg * 256, 256)])

                  def ev_q(mi, tb, pap, pt, qg=qg):
                      ch = qg * 2 + mi
                      act(op("activation", out=X[:, ch, tb * TB:(tb + 1) * TB], in_=pap, func=AF.Copy, scale=512.0 ** -0.5),
                          [pt], [Xt[ch]])
                  proj(w3, wt, KC, [0, 128], lambda k, tb: U[:, k, tb * TB:(tb + 1) * TB], lambda k: [Ut[k]], ev_q)
              SFb = SF[:, :].bitcast(BF16)
              PTs = [SFb[:, tb * 1024:(tb + 1) * 1024].rearrange("p (j t) -> p j t", t=TB) for tb in range(NB)]
              RDs = [RS[:, tb * TB:(tb + 1) * TB] for tb in range(NB)]
              for tb in range(NB):
                  sl = slice(tb * TB, (tb + 1) * TB)
                  for mt in range(2):
                      bk = 2 * tb + mt
                      for dc in range(4):
                          mm(PM[:, bk, :], KTm[:, hd * 4 + dc, mt * 128:(mt + 1) * 128], X[:, dc, sl], dc == 0, dc == 3,
                             reads=tKT + [Xt[dc]], writes=[PMt[bk]], signal=(dc == 3))
                      act(op("activation", out=PTs[tb][:, mt, :], in_=PM[:, bk, :], func=AF.Exp), [PMt[bk]], [SFh[tb]])
              for tb in range(NB):
                  for mt in range(2):
                      mm(PA[:, tb, :], ONES[:, :], PTs[tb][:, mt, :], mt == 0, mt == 1, reads=[SFh[tb], CSTt], writes=[PAt[tb]],
                         signal=(mt == 1))
                  act(op("activation", out=PA[:, tb, :], in_=PA[:, tb, :], func=AF.Ln), [PAt[tb]], [PAt[tb]])
                  act(op("activation", out=RDs[tb], in_=PA[:, tb, :], func=AF.Exp, scale=-1.0), [PAt[tb]], [RSt])
              for tb in range(NB):
                  sl = slice(tb * TB, (tb + 1) * TB)
                  for dc in range(4):
                      ch = hd * 4 + dc
                      ob = 2 + dc % 2
                      for mt in range(2):
                          mm(PA[:, ob, :], VTM[:, mt, ch * 128:(ch + 1) * 128], PTs[tb][:, mt, :], mt == 0, mt == 1,
                             reads=tVT + [SFh[tb]], writes=[PAt[ob]], signal=(mt == 1))
                      dve(op("tensor_tensor", out=AB[:, ch, sl], in0=PA[:, ob, :], in1=RDs[tb], op=ALU.mult),
                          [PAt[ob], RSt], [ABt[ch]])
          for mg in range(8):
              w3, wt = load_w(KC, 256, [("w_xo", 0, mg * 256, 256)])
              proj(w3, wt, KC, [0, 128], lambda k, tb: AB[:, k, tb * TB:(tb + 1) * TB], lambda k: [ABt[k]], evac_resid(mg * 2))
          dump("H2", [])
          rmsnorm_U(pb + P_FFN)
          f0 = 0
          for bi, nb_ in enumerate(FFB):
              ABUF, tAB_ = (AB, ABt) if bi % 2 == 0 else (X, Xt)
              for i in range(nb_):
                  fc = f0 + i
                  w3, wt = load_w(KC, 256, [("w_ffi", 0, fc * 128, 128), ("w_ffi", 0, DFF + fc * 128, 128)])
                  st = P.pa_set
                  P.pa_set ^= 1
                  for k in range(KC):
                      for tb in range(NB):
                          mm(PA[:, st * 2 + tb, :], w3[:, k, 0:128], U[:, k, tb * TB:(tb + 1) * TB], k == 0, k == KC - 1,
                             reads=[wt, Ut[k]], writes=[PAt[st * 2 + tb]], signal=(k == KC - 1))
                  for k in range(KC):
                      for tb in range(NB):
                          mm(PM[:, 2 + tb, :], w3[:, k, 128:256], U[:, k, tb * TB:(tb + 1) * TB], k == 0, k == KC - 1,
                             reads=[wt, Ut[k]], writes=[PMt[2 + tb]], signal=(k == KC - 1))
                  for tb in range(NB):
                      sl = slice(tb * TB, (tb + 1) * TB)
                      act(op("activation", out=RS[:, sl], in_=PA[:, st * 2 + tb, :], func=AF.Silu),
                          [PAt[st * 2 + tb]], [RSt])
                      dve(op("tensor_tensor", out=ABUF[:, i, sl], in0=PM[:, 2 + tb, :], in1=RS[:, sl],
                                                                                  op=ALU.mult), [PMt[2 + tb], RSt], [tAB_[i]])
              for mg in range(8):
                  w3, wt = load_w(nb_, 256, [("w_ffo", f0 * 128, mg * 256, 256)])
                  proj(w3, wt, nb_, [0, 128], lambda k, tb, ABUF=ABUF: ABUF[:, k, tb * TB:(tb + 1) * TB],
                       lambda k, tAB_=tAB_: [tAB_[k]], evac_resid(mg * 2))
              f0 += nb_

    except StopBuild:
        final_norm = False
    if final_norm:
        norm_stats()
        norm_apply(P_FINAL, dst=H)
    last = None
    for k in range(KC):
        last = P.dma("sp", "d:out", op("dma_start", out=d_out[:, k, :], in_=H[:, k, :]), reads=[Ht[k]])
    P._deps("sp", [], [], [last])
    wpp = max([sp[1] + sp[2] * sp[3] for sp in P.wspecs] + [128])
    P.wpp = wpp
    d_w = nc.dram_tensor("wts", [nl, 128, wpp], F32, kind="ExternalInput").ap()
    wsrc.append(d_w)
    with nc.Block() as block:
        @block.tensor
        def _(e):
            for fn in P.q["pe"]:
                fn(e)

        @block.scalar
        def _(e):
            for fn in P.q["act"]:
                fn(e)

        @block.vector
        def _(e):
            for fn in P.q["dve"]:
                fn(e)

        @block.gpsimd
        def _(e):
            for fn in P.q["pool"]:
                fn(e)

        @block.sync
        def _(e):
            for fn in P.q["sp"]:
                fn(e)
    es.close()
    return nc, P


def _fm(a):
    t, dd = a.shape
    return np.ascontiguousarray(a.reshape(t, dd // 128, 128).transpose(2, 1, 0))


def _cols(v):
    return v.reshape(-1, 128).T


def _params(inp, half):
    par = np.zeros((128, NPC), np.float32)
    for l in range(DEPTH):
        b = l * PLC
        par[:, b + P_MIX:b + P_MIX + 16] = _cols(inp["norm_mix"][l])
        par[:, b + P_MEM:b + P_MEM + 16] = _cols(inp["norm_mem"][l])
        par[:, b + P_CROSS:b + P_CROSS + 16] = _cols(inp["norm_cross"][l])
        par[:, b + P_FFN:b + P_FFN + 16] = _cols(inp["norm_ffn"][l])
        par[:, b + P_PSC:b + P_PSC + 8] = _cols(inp["pool_scale"][l])
        par[:, b + P_HN:b + P_HN + 8] = _cols(inp["hgrn_norm"][l])
        par[:, b + P_LB:b + P_LB + 8] = _cols(inp["hgrn_lower_bounds"][l])
    par[:, P_FINAL:P_FINAL + 16] = _cols(inp["norm_final"])
    par[:, P_FLAG] = float(half)
    for g, w in enumerate(WINDOWS):
        for t in range(16):
            cnt = w if half == 1 else min(t + 1, w)
            par[:, P_INVC + g * 16 + t] = np.float32(1.0) / np.float32(cnt)
    return par


def _consts():
    c = np.zeros((128, 128 + 512 + 1024), np.float32)
    c[:, 0:128] = np.eye(128, dtype=np.float32)
    p = np.arange(128)[:, None] % 64
    t = np.arange(512)[None, :] % 64
    c[:, 128:640] = (p <= t).astype(np.float32)
    c[:, 640:1664] = (np.arange(1024)[None, :] % 64 != 0).astype(np.float32)
    return c


_WNAMES = {"w_in": "w_in", "w_pool": "w_pool_group", "w_bp": "w_branch_pool", "w_bh": "w_branch_hgrn", "w_mix": "w_mix_out",
           "w_xq": "w_xq", "w_xkv": "w_xkv", "w_xo": "w_xo", "w_ffi": "w_ffn_in", "w_ffo": "w_ffn_out"}


def _pack_weights(P, inp, layers):
    nl = len(layers)
    out = np.zeros((nl, 128, P.wpp), np.float32)
    for (li, off, kc, ncols, srcs) in P.wspecs:
        l = layers[li]
        c = off
        blk = out[li, :, off:off + kc * ncols].reshape(128, kc, ncols)
        cc = 0
        for (wn, r0, c0, nci) in srcs:
            W = inp[_WNAMES[wn]][l]
            if wn == "w_pool":
                W = W.reshape(1024, 256)
            blk[:, :, cc:cc + nci] = W[r0:r0 + kc * 128, c0:c0 + nci].reshape(kc, 128, nci).transpose(1, 0, 2)
            cc += nci
    return out


_CACHE = {}


def _get_prog(layers, final_norm, ncores, dbg=None):
    key = (tuple(layers), final_norm, ncores, dbg)
    if key not in _CACHE:
        _CACHE[key] = build(list(layers), final_norm, ncores, dbg)
    return _CACHE[key]


def run_launch(inp, h_fm, layers, final_norm, ncores=8, dbg=None):
    nc, P = _get_prog(layers, final_norm, ncores, dbg)
    wts = _pack_weights(P, inp, layers)
    cst = _consts()
    in_maps = []
    for c in range(ncores):
        b, half = c // 2, c % 2
        in_maps.append({"hin": h_fm[c], "memt": _fm(np.asarray(inp["mem"][b])), "par": _params(inp, half), "cst": cst, "wts": wts})
    res = run_bass_kernel_spmd(nc, in_maps, core_ids=list(range(ncores)))
    return [np.asarray(r["hout"]) for r in res.results]


FUSED = True


def kernel(**inputs):
    inp = {k: np.asarray(v) for k, v in inputs.items()}
    x = inp["x"]
    ncores = 8
    h = [_fm(x[c // 2, (c % 2) * NT:(c % 2 + 1) * NT, :]) for c in range(ncores)]
    if FUSED:
        h = run_launch(inp, h, list(range(DEPTH)), True, ncores)
    else:
        for l in range(DEPTH):
            h = run_launch(inp, h, [l], l == DEPTH - 1, ncores)
    out = np.zeros(x.shape, np.float32)
    for c in range(ncores):
        out[c // 2, (c % 2) * NT:(c % 2 + 1) * NT, :] = h[c].transpose(2, 1, 0).reshape(NT, D)
    return out
```
